# Optimizing a Trainium2 kernel written in Bass

```python
import math
import jax, jax.numpy as jnp
from jax import lax
import numpy as np

D_MODEL = 2048
BATCH = 4
SEQ = 2048
DEPTH = 1

MIX_WIDTH = D_MODEL
MLA_V = 128
MLA_HEADS = (MIX_WIDTH // 2) // MLA_V
MLA_NOPE = 128
MLA_ROPE = 64
MLA_QK = MLA_NOPE + MLA_ROPE
Q_LORA = D_MODEL // 4
KV_LORA = D_MODEL // 4
DIFF_V = 128
DIFF_HEADS = (MIX_WIDTH - MLA_HEADS * MLA_V) // DIFF_V
DIFF_QK = 64
DIFF_ROT = DIFF_QK // 4
ROPE_THETA = 500000.0
D_FF = -(-8 * D_MODEL // (3 * 256)) * 256
Q_BLOCK = 128
EPS = 1e-6

IN_SPLITS = (Q_LORA, KV_LORA, MLA_ROPE,
             DIFF_HEADS * 2 * DIFF_QK, DIFF_HEADS * 2 * DIFF_QK, DIFF_HEADS * DIFF_V)
IN_COLS = sum(IN_SPLITS)

kernel_name = "hybrid_mla_diffattn_parallel_heads"


def rmsnorm(x, g):
    xf = x.astype(jnp.float32)
    y = xf * lax.rsqrt(jnp.mean(xf * xf, axis=-1, keepdims=True) + EPS)
    return (y * g.astype(jnp.float32)).astype(x.dtype)


def rope(x, pos):
    r = x.shape[-1]
    half = r // 2
    freqs = 1.0 / (ROPE_THETA ** (jnp.arange(0, r, 2, dtype=jnp.float32) / r))
    ang = pos.astype(jnp.float32)[:, None] * freqs[None, :]
    extra = x.ndim - 3
    ang = ang.reshape((1, ang.shape[0]) + (1,) * extra + (half,))
    cos, sin = jnp.cos(ang), jnp.sin(ang)
    xf = x.astype(jnp.float32)
    x1, x2 = xf[..., :half], xf[..., half:]
    out = jnp.concatenate([x1 * cos - x2 * sin, x2 * cos + x1 * sin], axis=-1)
    return out.astype(x.dtype)


def causal_mask(i, seq):
    qpos = i * Q_BLOCK + jnp.arange(Q_BLOCK)
    return jnp.arange(seq)[None, :] <= qpos[:, None]


def mla_attend(q, k, v):
    B, H, S, D = q.shape
    nb = S // Q_BLOCK
    scale = 1.0 / math.sqrt(D)
    qb = q.reshape(B, H, nb, Q_BLOCK, D).transpose(2, 0, 1, 3, 4)

    def one_block(args):
        qi, i = args
        s = jnp.einsum('bhqd,bhkd->bhqk', qi, k).astype(jnp.float32) * scale
        s = jnp.where(causal_mask(i, S)[None, None], s, -jnp.inf)
        p = jax.nn.softmax(s, axis=-1)
        return jnp.einsum('bhqk,bhkd->bhqd', p.astype(v.dtype), v)

    out = lax.map(one_block, (qb, jnp.arange(nb)))
    return out.transpose(1, 0, 3, 2, 4).reshape(B, S, H, v.shape[-1])


def diff_attend(q, k, v, lam):
    B, H, _, S, D = q.shape
    nb = S // Q_BLOCK
    scale = 1.0 / math.sqrt(D)
    qb = q.reshape(B, H, 2, nb, Q_BLOCK, D).transpose(3, 0, 1, 2, 4, 5)

    def one_block(args):
        qi, i = args
        s = jnp.einsum('bhcqd,bhckd->bhcqk', qi, k).astype(jnp.float32) * scale
        s = jnp.where(causal_mask(i, S)[None, None, None], s, -jnp.inf)
        p = jax.nn.softmax(s, axis=-1)
        a = p[:, :, 0] - lam * p[:, :, 1]
        return jnp.einsum('bhqk,bhkd->bhqd', a.astype(v.dtype), v)

    out = lax.map(one_block, (qb, jnp.arange(nb)))
    return out.transpose(1, 0, 3, 2, 4).reshape(B, S, H, v.shape[-1])


def setup_inputs(seed: int = 0) -> dict:
    key = jax.random.key(seed)
    ks = jax.random.split(key, 24)
    f = jnp.float32

    def nrm(k, shape, fan_in):
        return jax.random.normal(k, shape, f) * (fan_in ** -0.5)

    def gain(k, n):
        return 1.0 + 0.02 * jax.random.normal(k, (DEPTH, n), f)

    return {
        "x": jax.random.normal(ks[0], (BATCH, SEQ, D_MODEL), f),
        "attn_norm": gain(ks[1], D_MODEL),
        "w_in": nrm(ks[2], (DEPTH, D_MODEL, IN_COLS), D_MODEL),
        "q_latent_norm": gain(ks[3], Q_LORA),
        "w_q_up": nrm(ks[4], (DEPTH, Q_LORA, MLA_HEADS * MLA_QK), Q_LORA),
        "kv_latent_norm": gain(ks[5], KV_LORA),
        "w_kv_up": nrm(ks[6], (DEPTH, KV_LORA, MLA_HEADS * (MLA_NOPE + MLA_V)), KV_LORA),
        "mla_q_norm": gain(ks[7], MLA_QK),
        "mla_k_norm": gain(ks[8], MLA_QK),
        "mla_out_norm": gain(ks[9], MLA_V),
        "diff_q_norm": gain(ks[10], DIFF_QK),
        "diff_k_norm": gain(ks[11], DIFF_QK),
        "lambda_q1": 0.1 * jax.random.normal(ks[12], (DEPTH, DIFF_QK), f),
        "lambda_k1": 0.1 * jax.random.normal(ks[13], (DEPTH, DIFF_QK), f),
        "lambda_q2": 0.1 * jax.random.normal(ks[14], (DEPTH, DIFF_QK), f),
        "lambda_k2": 0.1 * jax.random.normal(ks[15], (DEPTH, DIFF_QK), f),
        "diff_out_norm": gain(ks[16], DIFF_V),
        "w_o": nrm(ks[17], (DEPTH, MIX_WIDTH, D_MODEL), MIX_WIDTH),
        "ffn_norm": gain(ks[18], D_MODEL),
        "w_gate": nrm(ks[19], (DEPTH, D_MODEL, D_FF), D_MODEL),
        "w_up": nrm(ks[20], (DEPTH, D_MODEL, D_FF), D_MODEL),
        "w_down": nrm(ks[21], (DEPTH, D_FF, D_MODEL), D_FF),
    }


def reference(x, attn_norm, w_in, q_latent_norm, w_q_up, kv_latent_norm, w_kv_up,
              mla_q_norm, mla_k_norm, mla_out_norm, diff_q_norm, diff_k_norm,
              lambda_q1, lambda_k1, lambda_q2, lambda_k2, diff_out_norm, w_o,
              ffn_norm, w_gate, w_up, w_down):
    B, S, _ = x.shape
    pos = jnp.arange(S, dtype=jnp.int32)
    offs = list(np.cumsum(IN_SPLITS)[:-1])

    for l in range(DEPTH):
        lambda_init = 0.8 - 0.6 * math.exp(-0.3 * l)

        h = rmsnorm(x, attn_norm[l])
        proj = h @ w_in[l]
        c_q, c_kv, k_pe, dq, dk, dv = jnp.split(proj, offs, axis=-1)

        q = (rmsnorm(c_q, q_latent_norm[l]) @ w_q_up[l]).reshape(B, S, MLA_HEADS, MLA_QK)
        kv = (rmsnorm(c_kv, kv_latent_norm[l]) @ w_kv_up[l]).reshape(
            B, S, MLA_HEADS, MLA_NOPE + MLA_V)
        k_nope, v_mla = kv[..., :MLA_NOPE], kv[..., MLA_NOPE:]
        k_pe = jnp.broadcast_to(k_pe[:, :, None, :], (B, S, MLA_HEADS, MLA_ROPE))
        k = jnp.concatenate([k_nope, k_pe], axis=-1)
        q = rmsnorm(q, mla_q_norm[l])
        k = rmsnorm(k, mla_k_norm[l])
        q = jnp.concatenate([q[..., :MLA_NOPE], rope(q[..., MLA_NOPE:], pos)], axis=-1)
        k = jnp.concatenate([k[..., :MLA_NOPE], rope(k[..., MLA_NOPE:], pos)], axis=-1)
        o_mla = mla_attend(q.transpose(0, 2, 1, 3), k.transpose(0, 2, 1, 3),
                           v_mla.transpose(0, 2, 1, 3))
        o_mla = rmsnorm(o_mla, mla_out_norm[l]).reshape(B, S, MLA_HEADS * MLA_V)

        dq = rmsnorm(dq.reshape(B, S, DIFF_HEADS, 2, DIFF_QK), diff_q_norm[l])
        dk = rmsnorm(dk.reshape(B, S, DIFF_HEADS, 2, DIFF_QK), diff_k_norm[l])
        dq = jnp.concatenate([rope(dq[..., :DIFF_ROT], pos), dq[..., DIFF_ROT:]], axis=-1)
        dk = jnp.concatenate([rope(dk[..., :DIFF_ROT], pos), dk[..., DIFF_ROT:]], axis=-1)
        lam = (jnp.exp(jnp.sum(lambda_q1[l].astype(jnp.float32) * lambda_k1[l].astype(jnp.float32)))
               - jnp.exp(jnp.sum(lambda_q2[l].astype(jnp.float32) * lambda_k2[l].astype(jnp.float32)))
               + lambda_init)
        dv = dv.reshape(B, S, DIFF_HEADS, DIFF_V).transpose(0, 2, 1, 3)
        o_diff = diff_attend(dq.transpose(0, 2, 3, 1, 4), dk.transpose(0, 2, 3, 1, 4), dv, lam)
        o_diff = (rmsnorm(o_diff, diff_out_norm[l]) * (1.0 - lambda_init)).astype(x.dtype)
        o_diff = o_diff.reshape(B, S, DIFF_HEADS * DIFF_V)

        x = x + jnp.concatenate([o_mla, o_diff], axis=-1) @ w_o[l]

        h = rmsnorm(x, ffn_norm[l])
        x = x + (jax.nn.silu(h @ w_gate[l]) * (h @ w_up[l])) @ w_down[l]

    return x
```

```python
import math
from contextlib import ExitStack

import numpy as np
import concourse.bass as bass
import concourse.mybir as mybir
from concourse.bass_utils import run_bass_kernel_spmd

F32 = mybir.dt.float32
BF16 = mybir.dt.bfloat16
AF = mybir.ActivationFunctionType
ALU = mybir.AluOpType
AX = mybir.AxisListType

D = 2048
S = 2048
NH = 8
DFF = 5632
NFC = DFF // 128
EPS = 1e-6
LAMBDA_INIT = 0.8 - 0.6 * math.exp(-0.3 * 0)
G0 = [0, 3, 4, 7, 8, 11, 12, 15]
G1 = [1, 2, 5, 6, 9, 10, 13, 14]

V_GQL, V_GKVL, V_GQ, V_GK, V_GMO, V_GDQ, V_GDK, V_LQ1, V_LK1, V_LQ2, V_LK2, V_GDO = (
    0, 512, 1024, 1216, 1408, 1536, 1600, 1664, 1728, 1792, 1856, 1920)
NVEC = 2048
R_CM, R_SM, R_CD, R_SD = 0, 32, 64, 72
NROPE = 80
C_VECS, C_GFFN, C_GCOL, C_MASK, C_IDENT, C_ROPE = 0, 2048, 4096, 4112, 4496, 4624
NCONST = C_ROPE + 16 * NROPE
_WSHAPES = [("wa", 2048, 1088), ("wd", 2048, 8 * 384), ("wqup", 512, 1536), ("wkvup", 512, 2048),
            ("wo", 2048, 2048), ("wg", 2048, 5632), ("wu", 2048, 5632), ("wdn", 5632, 2048)]
WTS_OFF = {}
_o = 0
for _n, _r, _c in _WSHAPES:
    WTS_OFF[_n] = (_o, _r, _c)
    _o += _r * _c
WTS_TOTAL = _o


class Slot:
    __slots__ = ("name", "w", "r", "excl")

    def __init__(self, name="", excl=False):
        self.name = name
        self.w = None
        self.r = {}
        self.excl = excl


class Prog:
    ENGS = ("pe", "act", "dve", "pool", "sp")

    def __init__(self, nc, es, ring=16):
        self.nc = nc
        self.lists = {e: [] for e in self.ENGS}
        self.sems = {}
        self.cnt = {}
        self.seen = {e: {} for e in self.ENGS}
        for e in ("pe", "act", "dve", "pool"):
            self.sems[e] = es.enter_context(nc.semaphore("c_" + e))
            self.cnt[e] = 0
        self.rings = {}
        self.ridx = {}
        for q in ("sp", "pool"):
            names = []
            for i in range(ring):
                n = "d_%s%d" % (q, i)
                self.sems[n] = es.enter_context(nc.semaphore(n))
                self.cnt[n] = 0
                names.append(n)
            self.rings[q] = names
            self.ridx[q] = 0

    skip = False
    defer = None

    def deferred(self, f):
        self.defer = []
        f()
        L = self.defer
        self.defer = None
        return L

    def release(self, L, n=None):
        k = len(L) if n is None else min(n, len(L))
        for _ in range(k):
            a = L.pop(0)
            self.emit(*a)

    def emit(self, eng, fn, reads=(), writes=(), dma=False):
        if self.skip:
            return None
        if self.defer is not None:
            self.defer.append((eng, fn, list(reads), list(writes), dma))
            return None
        ex = [x for x in reads if x.excl]
        if ex:
            reads = [x for x in reads if not x.excl]
            writes = list(writes) + ex
        deps = []
        for s in reads:
            if s.w is not None:
                deps.append(s.w)
        for s in writes:
            if s.w is not None:
                deps.append(s.w)
            deps.extend(s.r.items())
        if dma:
            ring = self.rings[eng]
            i = self.ridx[eng]
            self.ridx[eng] = (i + 1) % len(ring)
            sem = ring[i]
            if self.cnt[sem] > 0:
                deps.append((sem, self.cnt[sem]))
            inc = 16
        else:
            sem = eng
            inc = 1
        val = self.cnt[sem] + inc
        self.cnt[sem] = val
        need = {}
        for (s, v) in deps:
            if eng == "pe" and s == "pe":
                continue
            if self.seen[eng].get(s, 0) >= v:
                continue
            if need.get(s, 0) < v:
                need[s] = v
        for s, v in need.items():
            self.seen[eng][s] = v
        self.lists[eng].append((list(need.items()), fn, sem, inc))
        ev = (sem, val)
        for s in reads:
            if s.r.get(sem, 0) < val:
                s.r[sem] = val
        for s in writes:
            s.w = ev
            s.r = {}
        return ev

    def barrier(self, engines=None):
        if self.skip:
            return
        for e in (engines or self.ENGS):
            need = []
            for s, v in self.cnt.items():
                if v > 0 and self.seen[e].get(s, 0) < v:
                    if e == "pe" and s == "pe":
                        continue
                    need.append((s, v))
                    self.seen[e][s] = v
            if need:
                self.lists[e].append((need, None, None, 0))

    def replay(self, block):
        def run(e, L):
            for waits, fn, sem, inc in L:
                for s, v in waits:
                    e.wait_ge(self.sems[s], v)
                if fn is not None:
                    ins = fn(e)
                    ins.then_inc(self.sems[sem], inc)

        @block.tensor
        def _(e):
            run(e, self.lists["pe"])

        @block.scalar
        def _(e):
            run(e, self.lists["act"])

        @block.vector
        def _(e):
            run(e, self.lists["dve"])

        @block.gpsimd
        def _(e):
            run(e, self.lists["pool"])

        @block.sync
        def _(e):
            run(e, self.lists["sp"])


class Arena:
    def __init__(self, tensor, nbytes):
        self.t = tensor
        self.nbytes = nbytes
        self.top = 0
        self.peak = 0

    def alloc(self, shape, dtype, at=None):
        esz = 4 if dtype == F32 else 2
        n = 1
        for s in shape[1:]:
            n *= s
        nb = (n * esz + 63) // 64 * 64
        if at is None:
            off = self.top
            self.top += nb
        else:
            off = at
        assert off + nb <= self.nbytes, ("arena overflow", off, nb, self.nbytes)
        self.peak = max(self.peak, off + nb)
        ap = self.t[:, off // 2:(off + n * esz) // 2]
        if dtype == F32:
            ap = ap.bitcast(F32)
        if len(shape) == 3:
            ap = ap.rearrange("p (a b) -> p a b", a=shape[1])
        elif len(shape) == 4:
            ap = ap.rearrange("p (a b c) -> p a b c", a=shape[1], b=shape[2])
        if shape[0] < 128:
            ap = ap[0:shape[0]]
        return ap


def bc(ap, shape):
    return ap.to_broadcast(list(shape))


def build_nc(stage="full"):
    nc = bass.Bass("TRN2", target_bir_lowering=False)
    dram = lambda n, sh, kind="ExternalInput": nc.dram_tensor(n, list(sh), F32, kind=kind).ap()
    xT_d = dram("xT", [D, S])
    xtm_d = dram("xtm", [S, D])
    consts_d = dram("consts", [128, NCONST])
    vecs_d = consts_d[:, C_VECS:C_VECS + NVEC]
    gffn_d = consts_d[:, C_GFFN:C_GFFN + D]
    gcol_d = consts_d[:, C_GCOL:C_GCOL + 16]
    masks_d = consts_d[:, C_MASK:C_MASK + 384].rearrange("p (a b) -> p a b", a=3)
    ident_d = consts_d[:, C_IDENT:C_IDENT + 128]
    rope_d = consts_d[:, C_ROPE:C_ROPE + 16 * NROPE].rearrange("p (i c) -> p i c", i=16)
    wts_d = nc.dram_tensor("wts", [WTS_TOTAL], F32, kind="ExternalInput").ap()

    def wview(name):
        off, r, c = WTS_OFF[name]
        return wts_d[off:off + r * c].rearrange("(r c) -> r c", c=c)

    wa_d = wview("wa")
    wd_d = wview("wd").rearrange("r (h n) -> r h n", h=NH)
    wqup_d = wview("wqup")
    wkvup_d = wview("wkvup")
    wo_d = wview("wo")
    wg_d = wview("wg")
    wu_d = wview("wu")
    wdn_d = wview("wdn")
    out_d = dram("out", [1024, D], kind="ExternalOutput")

    es = ExitStack()
    with es:
        ARENA_BYTES = 204 * 1024
        arena_t = es.enter_context(nc.sbuf_tensor("arena", [128, ARENA_BYTES // 2], BF16))
        ps_t = es.enter_context(nc.psum_tensor("ps", [128, 8, 512], F32))
        P = Prog(nc, es)
        A = Arena(arena_t, ARENA_BYTES)
        KB = 1024

        psf = [ps_t[:, b, :] for b in range(8)]
        psb = [ps_t[:, b, :].bitcast(BF16) for b in range(8)]
        bank = [Slot("bank%d" % b, excl=True) for b in range(8)]
        out_slot = Slot("out")

        vecs = A.alloc([128, NVEC], F32)
        ropet = A.alloc([128, 16, NROPE], F32)
        masks = A.alloc([128, 3, 128], BF16)
        ident = A.alloc([128, 128], BF16)
        gcol = A.alloc([128, 16], F32)
        ssx = A.alloc([128, 16], F32)
        rx = A.alloc([128, 16], F32)
        lamc = A.alloc([128, 8], F32)
        small = A.alloc([128, 256], F32)
        P_TOP = A.top
        OCAT_OFF = P_TOP
        ocat = A.alloc([128, 8, D], BF16)
        BASE = A.top
        s_const = Slot("const")
        s_ssx = Slot("ssx")
        s_rx = Slot("rx")
        s_lam = Slot("lam")
        s_small = Slot("small")
        s_ocat = Slot("ocat")

        def dma(q, out, in_, reads=(), writes=()):
            return P.emit(q, lambda e: e.dma_start(out=out, in_=in_), reads, writes, dma=True)

        def act(out, in_, func, reads, writes, scale=1.0, bias=0.0, accum_out=None):
            kw = {}
            if accum_out is not None:
                kw["accum_out"] = accum_out
            if func == AF.Copy:
                return P.emit("act", lambda e: e.activation(out=out, in_=in_, func=func, scale=scale, **kw), reads, writes)
            return P.emit("act", lambda e: e.activation(out=out, in_=in_, func=func, scale=scale, bias=bias, **kw),
                          reads, writes)

        def tt(out, in0, in1, op, reads, writes, eng="dve"):
            return P.emit(eng, lambda e: e.tensor_tensor(out=out, in0=in0, in1=in1, op=op), reads, writes)

        def ts(out, in0, s1, op0, reads, writes, s2=None, op1=None, eng="dve"):
            if op1 is None:
                return P.emit(eng, lambda e: e.tensor_scalar(out=out, in0=in0, scalar1=s1, scalar2=None, op0=op0),
                              reads, writes)
            return P.emit(eng, lambda e: e.tensor_scalar(out=out, in0=in0, scalar1=s1, scalar2=s2, op0=op0, op1=op1),
                          reads, writes)

        def cp(out, in_, reads, writes, eng="dve"):
            if eng == "act":
                return act(out, in_, AF.Copy, reads, writes)
            return P.emit(eng, lambda e: e.tensor_copy(out=out, in_=in_), reads, writes)

        def red(out, in_, reads, writes):
            return P.emit("dve", lambda e: e.tensor_reduce(out=out, in_=in_, axis=AX.X, op=ALU.add), reads, writes)

        def rsqrt(out, in_, scale, reads, writes, tmp, tmp_slot):
            act(tmp, in_, AF.Ln, reads, [tmp_slot], scale=scale, bias=EPS)
            act(out, tmp, AF.Exp, [tmp_slot], writes, scale=-0.5)

        def mm_group(mms, reads, writes):
            def fn(e):
                ins = None
                for m in mms:
                    kw = {}
                    if len(m) > 5 and m[5]:
                        kw["skip_group_check"] = True
                    ins = e.matmul(m[0], lhsT=m[1], rhs=m[2], start=m[3], stop=m[4], **kw)
                return ins
            return P.emit("pe", fn, reads, writes)

        def tr_group(trs, reads, writes):
            def fn(e):
                ins = None
                for (o, i) in trs:
                    ins = e.transpose(o, i, ident)
                return ins
            return P.emit("pe", fn, list(reads) + [s_const], writes)

        if stage != "full":
            P.emit("pool", lambda e: e.memset(ocat, 0.0), [], [s_ocat])
        dma("sp", vecs, vecs_d, writes=[s_const])
        dma("sp", ropet, rope_d, writes=[s_const])
        dma("sp", gcol, gcol_d, writes=[s_const])
        dma("pool", masks, masks_d, writes=[s_const])
        dma("pool", ident, ident_d, writes=[s_const])

        sm = small
        tt(sm[:, 0:64], vecs[:, V_LQ1:V_LQ1 + 64], vecs[:, V_LK1:V_LK1 + 64], ALU.mult, [s_const], [s_small])
        tt(sm[:, 64:128], vecs[:, V_LQ2:V_LQ2 + 64], vecs[:, V_LK2:V_LK2 + 64], ALU.mult, [s_const], [s_small])
        red(sm[:, 128:130], sm[:, 0:128].rearrange("p (a b) -> p a b", a=2), [s_small], [s_small])
        act(sm[:, 130:132], sm[:, 128:130], AF.Exp, [s_small], [s_small])
        tt(lamc[:, 0:1], sm[:, 130:131], sm[:, 131:132], ALU.subtract, [s_small], [s_lam])
        ts(lamc[:, 0:1], lamc[:, 0:1], LAMBDA_INIT, ALU.add, [s_lam], [s_lam])
        ts(lamc[:, 1:2], lamc[:, 0:1], -1.0, ALU.mult, [s_lam], [s_lam])

        if stage == "p0":
            P.skip = True
        s_ssxg = [Slot() for _ in range(4)]
        s_rxg = [Slot() for _ in range(4)]

        def srx(i):
            return s_rxg[i // 4]

        A.top = BASE
        cqnT = A.alloc([128, 4, 1024], BF16)
        ckvnT = A.alloc([128, 4, S], BF16)
        kr_tm = A.alloc([128, 16, 64], F32)
        sspe = A.alloc([128, 16], F32)
        LAT_TOP = A.top
        s_cqnT = Slot("cqnT")
        s_ckvnT = Slot("ckvnT")
        s_kr = Slot("kr")
        s_sspe = Slot("sspe")

        xTg = [A.alloc([128, 16, 512], BF16) for _ in range(2)]
        sqT = A.alloc([128, 16, 512], BF16)
        ones2 = A.alloc([128, 2], BF16)
        s_sqT = Slot()
        s_ones = Slot()
        P.emit("dve", lambda e: e.memset(ones2, 1.0), [], [s_ones])

        def stats_group(g, xb, sxb):
            for c in range(16):
                act(sqT[:, c, :], xb[:, c, :], AF.Square, [sxb], [s_sqT])
            b_ = next_bank()
            for tl in range(4):
                mm_group([(psf[b_][:, 2 * tl:2 * tl + 2], sqT[:, c, tl * 128:(tl + 1) * 128], ones2, c == 0, c == 15)
                          for c in range(16)], [s_sqT, s_ones], [bank[b_]])
            cp(ssx[:, g * 4:g * 4 + 4], psf[b_][:, 0:8].rearrange("p (t k) -> p t k", k=2)[:, :, 0], [bank[b_]], [s_ssxg[g]])
            rsqrt(rx[:, g * 4:g * 4 + 4], ssx[:, g * 4:g * 4 + 4], 1.0 / D, [s_ssxg[g]], [s_rxg[g]],
                  sm[:, 200 + g * 4:204 + g * 4], s_small)
        wql = A.alloc([128, 16, 512], BF16)
        wkvl = A.alloc([128, 16, 512], BF16)
        wpe = A.alloc([128, 16, 64], BF16)
        cf = [A.alloc([128, 4, 512], F32) for _ in range(2)]
        cbf = A.alloc([128, 512], BF16)
        sq = A.alloc([128, 2048], F32)
        kpe_tm = A.alloc([128, 16, 64], F32)
        rtmp = A.alloc([128, 16, 64], F32)
        s_xTg = [Slot("xTg0"), Slot("xTg1")]
        s_cf = [Slot("cf0"), Slot("cf1")]
        s_cbf = Slot("cbf")
        s_sq = Slot("sq")
        s_kpe = Slot("kpe")
        s_rtmp = Slot("rtmp")

        wa_v = wa_d.rearrange("(c p) n -> p c n", p=128)
        xT_v = xT_d.rearrange("(c p) t -> p c t", p=128)
        s_wkvl = Slot()
        s_wql = Slot()
        s_wpe = Slot()
        dma("pool", wkvl, wa_v[:, :, 512:1024], writes=[s_wkvl])

        def load_xT_group(dst, s_dst, t0, n):
            dma("pool", dst, xT_v[:, :, t0:t0 + n], writes=[s_dst])

        def scale_xT(dst, s_dst, n):
            for c in range(16):
                ts(dst[:, c, 0:n], dst[:, c, 0:n], gcol[:, c:c + 1], ALU.mult, [s_const, s_dst], [s_dst])

        bk = [0]

        def next_bank():
            b = bk[0]
            bk[0] = (b + 1) % 8
            return b

        load_xT_group(xTg[0], s_xTg[0], 0, 512)
        dma("pool", wpe, wa_v[:, :, 1024:1088], writes=[s_wpe])
        dma("pool", wql, wa_v[:, :, 0:512], writes=[s_wql])
        for tg in range(4):
            xb = xTg[tg % 2]
            sxb = s_xTg[tg % 2]
            stats_group(tg, xb, sxb)
            if tg + 1 < 4:
                load_xT_group(xTg[(tg + 1) % 2], s_xTg[(tg + 1) % 2], (tg + 1) * 512, 512)
            scale_xT(xb, sxb, 512)
            kinds = ["kv"] + (["q"] if tg < 2 else [])
            for kind in kinds:
                W = wkvl if kind == "kv" else wql
                s_wl = s_wkvl if kind == "kv" else s_wql
                cfi = 0 if kind == "kv" else 1
                for tl in range(4):
                    i = tg * 4 + tl
                    b = next_bank()
                    mm_group([(psf[b], xb[:, c, tl * 128:(tl + 1) * 128], W[:, c, :], c == 0, c == 15) for c in range(16)],
                             [sxb, s_wl], [bank[b]])
                    act(cf[cfi][:, tl, :], psf[b], AF.Copy, [bank[b], srx(i)], [s_cf[cfi]], scale=rx[:, i:i + 1])
                    if kind == "kv":
                        b2 = next_bank()
                        mm_group([(psf[b2][:, 0:64], xb[:, c, tl * 128:(tl + 1) * 128], wpe[:, c, :], c == 0, c == 15)
                                  for c in range(16)], [sxb, s_wpe], [bank[b2]])
                        act(kpe_tm[:, i, :], psf[b2][:, 0:64], AF.Copy, [bank[b2], srx(i)], [s_kpe], scale=rx[:, i:i + 1])
                tt(sq.rearrange("p (a b) -> p a b", a=4), cf[cfi], cf[cfi], ALU.mult, [s_cf[cfi]], [s_sq])
                red(sm[:, 16:20], sq.rearrange("p (a b) -> p a b", a=4), [s_sq], [s_small])
                rsqrt(sm[:, 20:24], sm[:, 16:20], 1.0 / 512, [s_small], [s_small], sm[:, 24:28], s_small)
                gv = vecs[:, V_GKVL:V_GKVL + 512] if kind == "kv" else vecs[:, V_GQL:V_GQL + 512]
                dstT = ckvnT if kind == "kv" else cqnT
                s_dstT = s_ckvnT if kind == "kv" else s_cqnT
                for tl in range(4):
                    i = tg * 4 + tl
                    P.emit("dve", lambda e, tl=tl, cfi=cfi, gv=gv: e.scalar_tensor_tensor(
                        out=cbf, in0=cf[cfi][:, tl, :], scalar=sm[:, 20 + tl:21 + tl], in1=gv,
                        op0=ALU.mult, op1=ALU.mult), [s_cf[cfi], s_small, s_const], [s_cbf])
                    b = next_bank()
                    tr_group([(psb[b][:, c * 128:(c + 1) * 128], cbf[:, c * 128:(c + 1) * 128]) for c in range(4)],
                             [s_cbf], [bank[b]])
                    cp(dstT[:, :, i * 128:(i + 1) * 128], psb[b][:, 0:512].rearrange("p (c t) -> p c t", c=4),
                       [bank[b]], [s_dstT])

        tt(sq[:, 0:1024].rearrange("p (a b) -> p a b", a=16), kpe_tm, kpe_tm, ALU.mult, [s_kpe], [s_sq])
        red(sspe, sq[:, 0:1024].rearrange("p (a b) -> p a b", a=16), [s_sq], [s_sspe])
        gk_r = vecs[:, V_GK + 128:V_GK + 192]
        tt(kpe_tm, kpe_tm, bc(gk_r.unsqueeze(1), [128, 16, 64]), ALU.mult, [s_kpe, s_const], [s_kpe])
        cosm = ropet[:, :, R_CM:R_CM + 32]
        sinm = ropet[:, :, R_SM:R_SM + 32]

        def rope_tm(dst, s_dstl, src, s_srcl, tmp, s_tmpl, ntile, cos, sin, half):
            x1 = src[:, 0:ntile, 0:half]
            x2 = src[:, 0:ntile, half:2 * half]
            ta = tmp[:, 0:ntile, 0:half]
            tb = tmp[:, 0:ntile, half:2 * half]
            c_ = cos[:, 0:ntile, :]
            s_ = sin[:, 0:ntile, :]
            tt(ta, x1, c_, ALU.mult, s_srcl + [s_const], s_tmpl)
            tt(tb, x2, s_, ALU.mult, s_srcl + [s_const], s_tmpl)
            tt(dst[:, 0:ntile, 0:half], ta, tb, ALU.subtract, s_tmpl, s_dstl)
            tt(ta, x2, c_, ALU.mult, s_srcl + [s_const], s_tmpl)
            tt(tb, x1, s_, ALU.mult, s_srcl + [s_const], s_tmpl)
            tt(dst[:, 0:ntile, half:2 * half], ta, tb, ALU.add, s_tmpl, s_dstl)

        rope_tm(kr_tm, [s_kr], kpe_tm, [s_kpe], rtmp, [s_rtmp], 16, cosm, sinm, 32)
        P.barrier()
        if stage == "p2":
            P.skip = True

        _ptd = {}

        def ptd(slot):
            return _ptd.setdefault(id(slot), Slot())

        def attention(parts, vx, s_in, scale, oacc_banks, st_banks, PT, s_PT, fill=None, nfill=0):
            steps = [(jp, w) for jp in range(8) for w in (0, 1)]
            oloc = []
            for j in range(8):
                bnk = oacc_banks[j // 3]
                oloc.append((bnk, (j % 3) * 129))
            first_in_bank = {}
            last_in_bank = {}
            for s_i, (jp, w) in enumerate(steps):
                for j in list(range(jp + 1, 8)) + [jp]:
                    bnk = oloc[j][0]
                    first_in_bank.setdefault(bnk, (s_i, j))
                    last_in_bank[bnk] = (s_i, j)

            def qk(s_i):
                jp, w = steps[s_i]
                kt = jp + 8 * w
                sb = st_banks[s_i % 2]
                q0 = jp * 128
                mms = []
                for c0 in range(q0, 1024, 512):
                    n = min(512, 1024 - c0)
                    bnk = sb[(c0 - q0) // 512]
                    for pi, (Kp, Qp) in enumerate(parts):
                        mms.append((psf[bnk][:, 0:n], Kp[:, kt * 128:(kt + 1) * 128], Qp[:, c0:c0 + n],
                                    pi == 0, pi == len(parts) - 1))
                mm_group(mms, s_in, [bank[sb[0]], bank[sb[1]]])
                Wd = 1024 - q0
                pb = s_i % len(PT)
                sd = ptd(s_PT[pb])
                if Wd > 512:
                    act(PT[pb][:, 0:512], psf[sb[0]], AF.Exp, [bank[sb[0]]], [s_PT[pb], sd], scale=scale)
                    act(PT[pb][:, 512:Wd], psf[sb[1]][:, 0:Wd - 512], AF.Exp, [bank[sb[1]]], [s_PT[pb]], scale=scale)
                else:
                    act(PT[pb][:, 0:Wd], psf[sb[0]][:, 0:Wd], AF.Exp, [bank[sb[0]]], [s_PT[pb], sd], scale=scale)
                mi = 0 if w == 0 else 1 + (jp % 2)
                tt(PT[pb][:, 0:128], PT[pb][:, 0:128], masks[:, mi, :], ALU.mult, [sd, s_const], [sd])

            def pv(s_i):
                jp, w = steps[s_i]
                kt = jp + 8 * w
                pb = s_i % len(PT)
                for js, slot in ((list(range(jp + 1, 8)), s_PT[pb]), ([jp], ptd(s_PT[pb]))):
                    if not js:
                        continue
                    mms = []
                    wr = set()
                    for j in js:
                        bnk, off = oloc[j]
                        wr.add(bnk)
                        mms.append((psf[bnk][:, off:off + 129], PT[pb][:, (j - jp) * 128:(j - jp + 1) * 128],
                                    vx[:, kt, 0:129], first_in_bank[bnk] == (s_i, j), last_in_bank[bnk] == (s_i, j), True))
                    mm_group(mms, [slot] + list(s_in), [bank[b_] for b_ in sorted(wr)])

            qk(0)
            for s_i in range(16):
                if s_i + 1 < 16:
                    qk(s_i + 1)
                pv(s_i)
                if fill:
                    P.release(fill, nfill)
            return oloc

        def evac_o(oacc_banks, o_f, s_of):
            for bi, bnk in enumerate(oacc_banks):
                nj = 3 if bi < 2 else 2
                cp(o_f[:, bi * 3:bi * 3 + nj, :], psf[bnk][:, 0:nj * 129].rearrange("p (a b) -> p a b", a=nj),
                   [bank[bnk]], [s_of], eng=("dve" if bi != 1 else "act"))

        ST_BANKS = [(0, 1), (2, 3)]
        OACC = [4, 5, 6]

        A.top = LAT_TOP
        knT = [A.alloc([128, S], BF16) for _ in range(2)]
        krT = [A.alloc([128, S], BF16) for _ in range(2)]
        vx = [A.alloc([128, 16, 130], BF16) for _ in range(2)]
        qnT = [A.alloc([128, 1024], BF16) for _ in range(2)]
        qrT = [A.alloc([128, 1024], BF16) for _ in range(2)]
        wkvh = [A.alloc([128, 4, 256], BF16) for _ in range(2)]
        wq_all = A.alloc([128, 4, 1536], BF16)
        wqh = [wq_all[:, :, h_ * 192:(h_ + 1) * 192] for h_ in range(NH)]
        knope_tm = A.alloc([128, 16, 128], F32)
        q_tm = A.alloc([128, 8, 192], F32)
        kn_bf = A.alloc([128, 16, 128], BF16)
        krh_bf = A.alloc([128, 16, 64], BF16)
        qn_bf = A.alloc([128, 8, 128], BF16)
        qr_bf = A.alloc([128, 8, 64], BF16)
        qg = A.alloc([128, 8, 64], F32)
        qtmp = A.alloc([128, 8, 64], F32)
        sq = A.alloc([128, 2048], F32)
        PT = [A.alloc([128, 1024], BF16) for _ in range(3)]
        o_f = A.alloc([128, 8, 129], F32)
        s_knT = [Slot(), Slot()]
        s_krT = [Slot(), Slot()]
        s_vx = [Slot(), Slot()]
        s_qnT = [Slot(), Slot()]
        s_qrT = [Slot(), Slot()]
        s_wkvh = [Slot(), Slot()]
        s_wqh = [Slot(), Slot()]
        s_wqall = Slot()
        s_knope = Slot()
        s_qtm = Slot()
        s_knbf = Slot()
        s_krh = Slot()
        s_qnbf = Slot()
        s_qrbf = Slot()
        s_qg = Slot()
        s_qtmp = Slot()
        s_sq = Slot()
        s_PT = [Slot(), Slot(), Slot()]
        s_of = Slot()

        import os
        VAR = os.environ.get("KVAR", "")
        for hb in range(2):
            P.emit("pool", lambda e, hb=hb: e.memset(krT[hb][64:128, :], 0.0), [], [s_krT[hb]])
            P.emit("pool", lambda e, hb=hb: e.memset(qrT[hb][64:128, :], 0.0), [], [s_qrT[hb]])
            if VAR != "nomemset":
                P.emit("dve" if VAR == "dvememset" else "pool", lambda e, hb=hb: e.memset(vx[hb][:, :, 128:130], 1.0), [], [s_vx[hb]])

        wkvup_v = wkvup_d.rearrange("(c p) n -> p c n", p=128)
        wqup_v = wqup_d.rearrange("(c p) n -> p c n", p=128)

        def load_mla_w(h):
            hb = h % 2
            if VAR == "nodma":
                return
            if VAR != "onlyq":
                dma("pool", wkvh[hb], wkvup_v[:, :, h * 256:(h + 1) * 256], writes=[s_wkvh[hb], s_ord])

        gk_n = vecs[:, V_GK:V_GK + 128]
        gq_n = vecs[:, V_GQ:V_GQ + 128]
        gq_r = vecs[:, V_GQ + 128:V_GQ + 192]
        gmo = vecs[:, V_GMO:V_GMO + 128]

        s_ord = Slot("dmaorder")
        if VAR == "e1":
            dma("pool", wq_all[:, :, 0:128], wkvup_v[:, :, 256:384], writes=[s_wqall, s_ord])
        elif VAR == "e2":
            dma("sp", wq_all.bitcast(F32), wqup_v[:, :, 0:768], writes=[s_wqall, s_ord])
        else:
            dma("pool", wq_all, wqup_v, writes=[s_wqall, s_ord])
        n_mla = NH if stage not in ("mla1", "mla1a", "mla1b", "mla1p1", "mla1p2", "mla1p0", "mla1pk") else 1
        def mla_proj(h):
            hb = h % 2
            for ip in range(8):
                b = next_bank()
                for t in range(2):
                    i = ip * 2 + t
                    mm_group([(psf[b][:, t * 256:(t + 1) * 256], ckvnT[:, c, i * 128:(i + 1) * 128], wkvh[hb][:, c, :],
                               c == 0, c == 3) for c in range(4)], [s_ckvnT, s_wkvh[hb]], [bank[b]])
                pv_ = psf[b].rearrange("p (t c) -> p t c", t=2)
                if VAR not in ("noevac", "noact"):
                    cp(knope_tm[:, ip * 2:ip * 2 + 2, :], pv_[:, :, 0:128], [bank[b]], [s_knope], eng="act")
                if VAR not in ("noevac", "nodve"):
                    cp(vx[hb][:, ip * 2:ip * 2 + 2, 0:128], pv_[:, :, 128:256], [bank[b]], [s_vx[hb]], eng="dve")
            if stage == "mla1pk":
                P.skip = True
            for ip in range(4):
                b = next_bank()
                for t in range(2):
                    i = ip * 2 + t
                    mm_group([(psf[b][:, t * 192:(t + 1) * 192], cqnT[:, c, i * 128:(i + 1) * 128], wqh[h][:, c, :],
                               c == 0, c == 3) for c in range(4)], [s_cqnT, s_wqall], [bank[b]])
                if VAR not in ("noevac", "noact"):
                    cp(q_tm[:, ip * 2:ip * 2 + 2, :], psf[b][:, 0:384].rearrange("p (t c) -> p t c", t=2),
                       [bank[b]], [s_qtm], eng="act")

        def mla_norm(h):
            hb = h % 2
            if stage == "mla1p1":
                P.skip = True
            sq3 = sq.rearrange("p (a b) -> p a b", a=16)
            tt(sq3, knope_tm, knope_tm, ALU.mult, [s_knope], [s_sq])
            red(sm[:, 32:48], sq3, [s_sq], [s_small])
            tt(sm[:, 32:48], sm[:, 32:48], sspe, ALU.add, [s_small, s_sspe], [s_small])
            rsqrt(sm[:, 48:64], sm[:, 32:48], 1.0 / 192, [s_small], [s_small], sm[:, 64:80], s_small)
            rk = sm[:, 48:64]
            tt(knope_tm, knope_tm, bc(rk.unsqueeze(2), [128, 16, 128]), ALU.mult, [s_knope, s_small], [s_knope])
            tt(kn_bf, knope_tm, bc(gk_n.unsqueeze(1), [128, 16, 128]), ALU.mult, [s_knope, s_const], [s_knbf])
            tt(krh_bf, kr_tm, bc(rk.unsqueeze(2), [128, 16, 64]), ALU.mult, [s_kr, s_small], [s_krh])
            if stage == "mla1p2":
                P.skip = True
            sq3q = sq[:, 0:1536].rearrange("p (a b) -> p a b", a=8)
            tt(sq3q, q_tm, q_tm, ALU.mult, [s_qtm], [s_sq])
            red(sm[:, 80:88], sq3q, [s_sq], [s_small])
            rsqrt(sm[:, 88:96], sm[:, 80:88], 1.0 / 192, [s_small], [s_small], sm[:, 96:104], s_small)
            rq = sm[:, 88:96]
            tt(q_tm, q_tm, bc(rq.unsqueeze(2), [128, 8, 192]), ALU.mult, [s_qtm, s_small], [s_qtm])
            tt(qn_bf, q_tm[:, :, 0:128], bc(gq_n.unsqueeze(1), [128, 8, 128]), ALU.mult, [s_qtm, s_const], [s_qnbf])
            tt(qg, q_tm[:, :, 128:192], bc(gq_r.unsqueeze(1), [128, 8, 64]), ALU.mult, [s_qtm, s_const], [s_qg])
            rope_tm(qr_bf, [s_qrbf], qg, [s_qg], qtmp, [s_qtmp], 8, cosm, sinm, 32)

        def mla_tr(h):
            hb = h % 2
            for g in range(2):
                b = next_bank()
                tr_group([(psb[b][:, t * 128:(t + 1) * 128], kn_bf[:, g * 8 + t, :]) for t in range(8)],
                         [s_knbf], [bank[b]])
                cp(knT[hb][:, g * 1024:(g + 1) * 1024], psb[b], [bank[b]], [s_knT[hb]], eng=("act" if g else "dve"))
                b = next_bank()
                tr_group([(psb[b][0:64, t * 128:(t + 1) * 128], krh_bf[:, g * 8 + t, :]) for t in range(8)],
                         [s_krh], [bank[b]])
                cp(krT[hb][0:64, g * 1024:(g + 1) * 1024], psb[b][0:64, :], [bank[b]], [s_krT[hb]],
                   eng=("dve" if g else "act"))
            b = next_bank()
            tr_group([(psb[b][:, t * 128:(t + 1) * 128], qn_bf[:, t, :]) for t in range(8)], [s_qnbf], [bank[b]])
            cp(qnT[hb], psb[b], [bank[b]], [s_qnT[hb]], eng="dve")
            b = next_bank()
            tr_group([(psb[b][0:64, t * 128:(t + 1) * 128], qr_bf[:, t, :]) for t in range(8)], [s_qrbf], [bank[b]])
            cp(qrT[hb][0:64, :], psb[b][0:64, :], [bank[b]], [s_qrT[hb]], eng="act")

        def mla_attn(h, fill, nfill):
            hb = h % 2
            if stage == "mla1a":
                P.skip = True
            attention([(knT[hb], qnT[hb]), (krT[hb], qrT[hb])], vx[hb],
                      [s_knT[hb], s_qnT[hb], s_krT[hb], s_qrT[hb], s_vx[hb]], 1.0 / math.sqrt(192.0),
                      OACC, ST_BANKS, PT, s_PT, fill=fill, nfill=nfill)
            evac_o(OACC, o_f, s_of)
            if stage == "mla1b":
                P.skip = True
            P.emit("dve", lambda e: e.reciprocal(out=sm[:, 104:112], in_=o_f[:, :, 128]), [s_of], [s_small])
            sq3o = sq[:, 0:1024].rearrange("p (a b) -> p a b", a=8)
            tt(sq3o, o_f[:, :, 0:128], o_f[:, :, 0:128], ALU.mult, [s_of], [s_sq])
            red(sm[:, 112:120], sq3o, [s_sq], [s_small])
            tt(sm[:, 112:120], sm[:, 112:120], sm[:, 104:112], ALU.mult, [s_small], [s_small])
            tt(sm[:, 112:120], sm[:, 112:120], sm[:, 104:112], ALU.mult, [s_small], [s_small])
            rsqrt(sm[:, 120:128], sm[:, 112:120], 1.0 / 128, [s_small], [s_small], sm[:, 128:136], s_small)
            tt(sm[:, 120:128], sm[:, 120:128], sm[:, 104:112], ALU.mult, [s_small], [s_small])
            tt(o_f[:, :, 0:128], o_f[:, :, 0:128], bc(sm[:, 120:128].unsqueeze(2), [128, 8, 128]), ALU.mult,
               [s_of, s_small], [s_of])
            tt(ocat[:, :, h * 128:(h + 1) * 128], o_f[:, :, 0:128], bc(gmo.unsqueeze(1), [128, 8, 128]), ALU.mult,
               [s_of, s_const], [s_ocat])

        load_mla_w(0)
        if n_mla > 1:
            load_mla_w(1)
        mla_proj(0)
        mla_norm(0)
        mla_tr(0)
        for h in range(n_mla):
            fill = []
            if h + 1 < n_mla:
                mla_proj(h + 1)
                if h + 2 < n_mla:
                    load_mla_w(h + 2)
                fill = P.deferred(lambda: mla_norm(h + 1))
            nfill = (len(fill) + 13) // 14
            mla_attn(h, fill, nfill)
            P.release(fill)
            if h + 1 < n_mla:
                mla_tr(h + 1)
        P.barrier()

        if stage not in ("mla1", "mla1a", "mla1b", "mla1p1", "mla1p2", "mla1p0", "mla1pk", "mla"):
            A.top = BASE
            xT = A.alloc([128, 16, S], BF16)
            wdh = [A.alloc([128, 16, 384], BF16) for _ in range(2)]
            dkT = [A.alloc([128, S], BF16) for _ in range(2)]
            dqT = [[A.alloc([128, 1024], BF16) for _c in range(2)] for _ in range(2)]
            dvx = [A.alloc([128, 16, 130], BF16) for _ in range(2)]
            dk_tm = A.alloc([128, 16, 128], F32)
            dq_tm = A.alloc([128, 8, 128], F32)
            dk_bf = A.alloc([128, 16, 128], BF16)
            dq_bf = A.alloc([128, 8, 128], BF16)
            dtmp = A.alloc([128, 16, 32], F32)
            sq = A.alloc([128, 2048], F32)
            PT = [A.alloc([128, 1024], BF16) for _ in range(3)]
            o1_f = A.alloc([128, 8, 129], F32)
            o2_f = A.alloc([128, 8, 129], F32)
            s_xT = Slot()
            s_wdh = [Slot(), Slot()]
            s_dkT = [Slot(), Slot()]
            s_dqT = [Slot(), Slot()]
            s_dvx = [Slot(), Slot()]
            s_dktm = Slot()
            s_dqtm = Slot()
            s_dkbf = Slot()
            s_dqbf = Slot()
            s_dtmp = Slot()
            s_sq = Slot()
            s_PT = [Slot(), Slot(), Slot()]
            s_o1 = Slot()
            s_o2 = Slot()
            for hb in range(2):
                P.emit("pool", lambda e, hb=hb: e.memset(dvx[hb][:, :, 128:130], 1.0), [], [s_dvx[hb]])
                P.emit("pool", lambda e, hb=hb: e.memset(dqT[hb][0][64:128, :], 0.0), [], [s_dqT[hb]])
                P.emit("pool", lambda e, hb=hb: e.memset(dqT[hb][1][0:64, :], 0.0), [], [s_dqT[hb]])
            wd_v = wd_d.rearrange("(c p) h n -> p c h n", p=128)

            def load_wd(h):
                dma("pool", wdh[h % 2], wd_v[:, :, h, :], writes=[s_wdh[h % 2]])

            for tg in range(4):
                dma("pool", xT[:, :, tg * 512:(tg + 1) * 512], xT_v[:, :, tg * 512:(tg + 1) * 512], writes=[s_xT])
            load_wd(0)
            for c in range(16):
                ts(xT[:, c, :], xT[:, c, :], gcol[:, c:c + 1], ALU.mult, [s_const, s_xT], [s_xT])
            gdq = vecs[:, V_GDQ:V_GDQ + 64]
            gdk = vecs[:, V_GDK:V_GDK + 64]
            gdo = vecs[:, V_GDO:V_GDO + 128]
            cosd = ropet[:, :, R_CD:R_CD + 8]
            sind = ropet[:, :, R_SD:R_SD + 8]

            def diff_norm_rope(src, s_src, dst, s_dst, ntile, gain, c0, c1):
                nsub = ntile * 2
                s4 = src.rearrange("p t (c d) -> p (t c) d", c=2)
                sqv = sq[:, 0:nsub * 64].rearrange("p (a b) -> p a b", a=nsub)
                tt(sqv, s4, s4, ALU.mult, [s_src], [s_sq])
                red(sm[:, c0:c0 + nsub], sqv, [s_sq], [s_small])
                rsqrt(sm[:, c1:c1 + nsub], sm[:, c0:c0 + nsub], 1.0 / 64, [s_small], [s_small],
                      sm[:, c0:c0 + nsub], s_small)
                tt(s4, s4, bc(sm[:, c1:c1 + nsub].unsqueeze(2), [128, nsub, 64]), ALU.mult, [s_src, s_small], [s_src])
                tt(s4, s4, bc(gain.unsqueeze(1), [128, nsub, 64]), ALU.mult, [s_src, s_const], [s_src])
                d4 = dst.rearrange("p t (c d) -> p (t c) d", c=2)
                cp(d4[:, :, 16:64], s4[:, :, 16:64], [s_src], [s_dst])
                s5 = src.rearrange("p t (c d) -> p t c d", c=2)
                d5 = dst.rearrange("p t (c d) -> p t c d", c=2)
                cs = bc(cosd[:, 0:ntile, :].unsqueeze(2), [128, ntile, 2, 8])
                sn = bc(sind[:, 0:ntile, :].unsqueeze(2), [128, ntile, 2, 8])
                t5 = dtmp[:, 0:ntile, :].rearrange("p t (c d) -> p t c d", c=2)
                x1 = s5[:, :, :, 0:8]
                x2 = s5[:, :, :, 8:16]
                tt(t5[:, :, :, 0:8], x1, cs, ALU.mult, [s_src, s_const], [s_dtmp])
                tt(t5[:, :, :, 8:16], x2, sn, ALU.mult, [s_src, s_const], [s_dtmp])
                tt(d5[:, :, :, 0:8], t5[:, :, :, 0:8], t5[:, :, :, 8:16], ALU.subtract, [s_dtmp], [s_dst])
                tt(t5[:, :, :, 0:8], x2, cs, ALU.mult, [s_src, s_const], [s_dtmp])
                tt(t5[:, :, :, 8:16], x1, sn, ALU.mult, [s_src, s_const], [s_dtmp])
                tt(d5[:, :, :, 8:16], t5[:, :, :, 0:8], t5[:, :, :, 8:16], ALU.add, [s_dtmp], [s_dst])

            n_diff = NH if stage != "diff1" else 1
            wo = A.alloc([128, 16, D], BF16, at=BASE)
            s_wo = [Slot() for _ in range(4)]
            wo_v = wo_d.rearrange("(c p) n -> p c n", p=128)
            def diff_proj(h):
                hb = h % 2
                for i in range(16):
                    b = next_bank()
                    lo = 0 if i < 8 else 128
                    n = 384 - lo
                    mm_group([(psf[b][:, 0:n], xT[:, c, i * 128:(i + 1) * 128], wdh[hb][:, c, lo:384], c == 0, c == 15)
                              for c in range(16)], [s_xT, s_wdh[hb]], [bank[b]])
                    o_ = 0
                    if i < 8:
                        act(dq_tm[:, i, :], psf[b][:, 0:128], AF.Copy, [bank[b], srx(i)], [s_dqtm], scale=rx[:, i:i + 1])
                        o_ = 128
                    act(dk_tm[:, i, :], psf[b][:, o_:o_ + 128], AF.Copy, [bank[b], srx(i)], [s_dktm], scale=rx[:, i:i + 1])
                    ts(dvx[hb][:, i, 0:128], psf[b][:, o_ + 128:o_ + 256], rx[:, i:i + 1], ALU.mult,
                       [bank[b], srx(i)], [s_dvx[hb]])

            def diff_norm(h):
                hb = h % 2
                diff_norm_rope(dk_tm, s_dktm, dk_bf, s_dkbf, 16, gdk, 32, 64)
                diff_norm_rope(dq_tm, s_dqtm, dq_bf, s_dqbf, 8, gdq, 96, 112)

            def diff_tr(h):
                hb = h % 2
                for g in range(2):
                    b = next_bank()
                    tr_group([(psb[b][:, t * 128:(t + 1) * 128], dk_bf[:, g * 8 + t, :]) for t in range(8)],
                             [s_dkbf], [bank[b]])
                    cp(dkT[hb][:, g * 1024:(g + 1) * 1024], psb[b], [bank[b]], [s_dkT[hb]], eng=("act" if g else "dve"))
                b = next_bank()
                tr_group([(psb[b][:, t * 128:(t + 1) * 128], dq_bf[:, t, :]) for t in range(8)], [s_dqbf], [bank[b]])
                cp(dqT[hb][0][0:64, :], psb[b][0:64, :], [bank[b]], [s_dqT[hb]], eng="dve")
                cp(dqT[hb][1][64:128, :], psb[b][64:128, :], [bank[b]], [s_dqT[hb]], eng="act")

            def diff_attn(h, fill, nfill):
                hb = h % 2
                for c in range(2):
                    attention([(dkT[hb], dqT[hb][c])], dvx[hb],
                              [s_dkT[hb], s_dqT[hb], s_dvx[hb]], 1.0 / 8.0, OACC, ST_BANKS, PT, s_PT, fill=fill, nfill=nfill)
                    evac_o(OACC, o1_f if c == 0 else o2_f, s_o1 if c == 0 else s_o2)
                P.emit("dve", lambda e: e.reciprocal(out=sm[:, 128:136], in_=o1_f[:, :, 128]), [s_o1], [s_small])
                P.emit("dve", lambda e: e.reciprocal(out=sm[:, 136:144], in_=o2_f[:, :, 128]), [s_o2], [s_small])
                ts(sm[:, 136:144], sm[:, 136:144], lamc[:, 1:2], ALU.mult, [s_small, s_lam], [s_small])
                d1 = o1_f[:, :, 0:128]
                d2 = o2_f[:, :, 0:128]
                tt(d1, d1, bc(sm[:, 128:136].unsqueeze(2), [128, 8, 128]), ALU.mult, [s_o1, s_small], [s_o1])
                tt(d2, d2, bc(sm[:, 136:144].unsqueeze(2), [128, 8, 128]), ALU.mult, [s_o2, s_small], [s_o2])
                tt(d1, d1, d2, ALU.add, [s_o1, s_o2], [s_o1])
                sq3o = sq[:, 0:1024].rearrange("p (a b) -> p a b", a=8)
                tt(sq3o, d1, d1, ALU.mult, [s_o1], [s_sq])
                red(sm[:, 144:152], sq3o, [s_sq], [s_small])
                rsqrt(sm[:, 152:160], sm[:, 144:152], 1.0 / 128, [s_small], [s_small], sm[:, 160:168], s_small)
                ts(sm[:, 152:160], sm[:, 152:160], 1.0 - LAMBDA_INIT, ALU.mult, [s_small], [s_small])
                tt(d1, d1, bc(sm[:, 152:160].unsqueeze(2), [128, 8, 128]), ALU.mult, [s_o1, s_small], [s_o1])
                tt(ocat[:, :, 1024 + h * 128:1024 + (h + 1) * 128], d1, bc(gdo.unsqueeze(1), [128, 8, 128]), ALU.mult,
                   [s_o1, s_const], [s_ocat])

            if n_diff > 1:
                load_wd(1)
            diff_proj(0)
            diff_norm(0)
            diff_tr(0)
            for h in range(n_diff):
                fill = []
                if h + 1 < n_diff:
                    diff_proj(h + 1)
                    if h + 2 < n_diff:
                        load_wd(h + 2)
                    elif stage == "full":
                        for g in range(4):
                            dma("pool", wo[:, :, g * 512:(g + 1) * 512], wo_v[:, :, g * 512:(g + 1) * 512],
                                writes=[s_wo[g], s_xT])
                    fill = P.deferred(lambda: diff_norm(h + 1))
                nfill = (len(fill) + 27) // 28
                diff_attn(h, fill, nfill)
                P.release(fill)
                if h + 1 < n_diff:
                    diff_tr(h + 1)
            P.barrier()

        P.skip = False
        if stage != "full":
            A.top = BASE
            stg = [A.alloc([128, D], F32) for _ in range(2)]
            s_stg = [Slot(), Slot()]
            for j in range(8):
                cp(stg[j % 2], ocat[:, j, :], [s_ocat], [s_stg[j % 2]])
                dma("sp", out_d[j * 128:(j + 1) * 128, :], stg[j % 2], reads=[s_stg[j % 2]], writes=[out_slot])
        else:
            A.top = BASE + 64 * 1024
            ocatT = A.alloc([128, 16, 1024], BF16)
            s_ocatT = Slot()
            for j in range(8):
                for g in range(2):
                    b = next_bank()
                    tr_group([(psb[b][:, t * 128:(t + 1) * 128], ocat[:, j, (g * 8 + t) * 128:(g * 8 + t + 1) * 128])
                              for t in range(8)], [s_ocat], [bank[b]])
                    cp(ocatT[:, g * 8:(g + 1) * 8, j * 128:(j + 1) * 128],
                       psb[b].rearrange("p (c t) -> p c t", c=8), [bank[b]], [s_ocatT],
                       eng=("act" if (j + g) % 2 else "dve"))
            P.barrier()
            h2T = A.alloc([128, 16, 1024], BF16, at=OCAT_OFF)
            s_h2T = Slot()
            xt_b = [A.alloc([128, D], F32) for _ in range(2)]
            x1_b = [A.alloc([128, D], F32) for _ in range(2)]
            h2_bf = [A.alloc([128, D], BF16) for _ in range(2)]
            gffn = A.alloc([128, D], F32)
            junk = A.alloc([128, D], BF16)
            s_xt = [Slot(), Slot()]
            s_x1 = [Slot(), Slot()]
            s_h2bf = [Slot(), Slot()]
            s_gffn = Slot()
            s_junk = Slot()
            dma("sp", gffn, gffn_d, writes=[s_gffn])
            dma("sp", xt_b[0], xtm_d[0:128, :], writes=[s_xt[0]])
            for j in range(8):
                jb = j % 2
                if j + 1 < 8:
                    dma("sp", xt_b[(j + 1) % 2], xtm_d[(j + 1) * 128:(j + 2) * 128, :], writes=[s_xt[(j + 1) % 2]])
                for g in range(4):
                    b = next_bank()
                    mm_group([(psf[b], ocatT[:, c, j * 128:(j + 1) * 128], wo[:, c, g * 512:(g + 1) * 512], c == 0, c == 15)
                              for c in range(16)], [s_ocatT, s_wo[g]], [bank[b]])
                    tt(x1_b[jb][:, g * 512:(g + 1) * 512], psf[b], xt_b[jb][:, g * 512:(g + 1) * 512], ALU.add,
                       [bank[b], s_xt[jb]], [s_x1[jb]])
                dma("sp", out_d[j * 128:(j + 1) * 128, :], x1_b[jb], reads=[s_x1[jb]], writes=[out_slot])
                act(junk, x1_b[jb], AF.Square, [s_x1[jb]], [s_junk, s_small], accum_out=sm[:, 168 + j:169 + j])
                rsqrt(sm[:, 176 + j:177 + j], sm[:, 168 + j:169 + j], 1.0 / D, [s_small], [s_small],
                      sm[:, 184 + j:185 + j], s_small)
                P.emit("dve", lambda e, jb=jb, j=j: e.scalar_tensor_tensor(
                    out=h2_bf[jb], in0=x1_b[jb], scalar=sm[:, 176 + j:177 + j], in1=gffn, op0=ALU.mult, op1=ALU.mult),
                    [s_x1[jb], s_small, s_gffn], [s_h2bf[jb]])
                for g in range(2):
                    b = next_bank()
                    tr_group([(psb[b][:, t * 128:(t + 1) * 128], h2_bf[jb][:, (g * 8 + t) * 128:(g * 8 + t + 1) * 128])
                              for t in range(8)], [s_h2bf[jb]], [bank[b]])
                    cp(h2T[:, g * 8:(g + 1) * 8, j * 128:(j + 1) * 128], psb[b].rearrange("p (c t) -> p c t", c=8),
                       [bank[b]], [s_h2T], eng=("act" if g else "dve"))
            P.barrier()
            A.top = BASE
            actT = A.alloc([128, NFC, 1024], BF16)
            s_actT = Slot()
            FG = 256
            NG = DFF // FG
            wgb = [A.alloc([128, 16, FG], BF16) for _ in range(2)]
            wub = [A.alloc([128, 16, FG], BF16) for _ in range(2)]
            sg = [A.alloc([128, 512], F32) for _ in range(2)]
            s_wgb = [Slot(), Slot()]
            s_wub = [Slot(), Slot()]
            s_sg = [Slot(), Slot()]
            G_TOP = A.top
            CG = 256
            wdb0 = A.alloc([128, NFC, CG], BF16)
            s_wdb4 = [[Slot() for _ in range(4)] for _ in range(2)]
            wdn_v = wdn_d.rearrange("(c p) n -> p c n", p=128)

            def load_dn_w(cg, buf):
                for q4 in range(4):
                    dma("pool", buf[:, q4 * 11:(q4 + 1) * 11, :], wdn_v[:, q4 * 11:(q4 + 1) * 11, cg * CG:(cg + 1) * CG],
                        writes=[s_wdb4[cg % 2][q4]])
            wg_v = wg_d.rearrange("(c p) n -> p c n", p=128)
            wu_v = wu_d.rearrange("(c p) n -> p c n", p=128)

            def load_gu(g):
                dma("pool", wgb[g % 2], wg_v[:, :, g * FG:(g + 1) * FG], writes=[s_wgb[g % 2]])
                dma("pool", wub[g % 2], wu_v[:, :, g * FG:(g + 1) * FG], writes=[s_wub[g % 2]])

            load_gu(0)
            k_ = 0
            for g in range(NG):
                gb = g % 2
                if g + 1 < NG:
                    load_gu(g + 1)
                else:
                    load_dn_w(0, wdb0)
                for fl in range(FG // 128):
                    f = g * (FG // 128) + fl
                    for th in range(2):
                        bg = next_bank()
                        bu = next_bank()
                        mm_group([(psf[bg], wgb[gb][:, c, fl * 128:(fl + 1) * 128], h2T[:, c, th * 512:(th + 1) * 512],
                                   c == 0, c == 15) for c in range(16)], [s_wgb[gb], s_h2T], [bank[bg]])
                        mm_group([(psf[bu], wub[gb][:, c, fl * 128:(fl + 1) * 128], h2T[:, c, th * 512:(th + 1) * 512],
                                   c == 0, c == 15) for c in range(16)], [s_wub[gb], s_h2T], [bank[bu]])
                        kb = k_ % 2
                        k_ += 1
                        act(sg[kb], psf[bg], AF.Silu, [bank[bg]], [s_sg[kb]])
                        tt(actT[:, f, th * 512:(th + 1) * 512], sg[kb], psf[bu], ALU.mult, [s_sg[kb], bank[bu]], [s_actT])
            P.barrier()
            A.top = G_TOP - (4 * 16 * FG * 2 + 2 * 2048)
            A.top = BASE + NFC * 1024 * 2
            NCG = D // CG
            wdb = [wdb0, A.alloc([128, NFC, CG], BF16)]
            x1s = [A.alloc([128, 8, CG], F32, at=OCAT_OFF + k * 8 * KB) for k in range(2)]
            ost = [A.alloc([128, 8, CG], F32, at=OCAT_OFF + (2 + k) * 8 * KB) for k in range(2)]
            s_x1s = [Slot(), Slot()]
            s_ost = [Slot(), Slot()]
            out_v = out_d.rearrange("(j p) n -> p j n", p=128)

            def load_dn(cg, w=True):
                if w:
                    load_dn_w(cg, wdb[cg % 2])
                dma("sp", x1s[cg % 2], out_v[:, :, cg * CG:(cg + 1) * CG], reads=[out_slot], writes=[s_x1s[cg % 2]])

            load_dn(0, w=False)
            for cg in range(NCG):
                cb = cg % 2
                if cg + 1 < NCG:
                    load_dn(cg + 1)
                for j in range(8):
                    b = next_bank()
                    mm_group([(psf[b][:, 0:CG], actT[:, f, j * 128:(j + 1) * 128], wdb[cb][:, f, :], f == 0, f == NFC - 1)
                              for f in range(NFC)], [s_actT] + s_wdb4[cb], [bank[b]])
                    tt(ost[cb][:, j, :], psf[b][:, 0:CG], x1s[cb][:, j, :], ALU.add, [bank[b], s_x1s[cb]], [s_ost[cb]])
                dma("sp", out_v[:, :, cg * CG:(cg + 1) * CG], ost[cb], reads=[s_ost[cb], s_x1s[cb]], writes=[out_slot])

        P.barrier(["sp"])
        block = es.enter_context(nc.Block())
        P.replay(block)
    return nc


def _prep_inputs(inp):
    f = np.float32
    x = np.asarray(inp["x"], f)
    w_in = np.asarray(inp["w_in"], f)[0]
    wa = np.ascontiguousarray(w_in[:, :1088])
    dq = w_in[:, 1088:2112].reshape(D, NH, 128)
    dk = w_in[:, 2112:3136].reshape(D, NH, 128)
    dv = w_in[:, 3136:4160].reshape(D, NH, 128)
    wd = np.ascontiguousarray(np.concatenate([dq, dk, dv], axis=2))
    vec = np.zeros(NVEC, f)

    def put(off, name):
        v = np.asarray(inp[name], f)[0]
        vec[off:off + v.shape[0]] = v

    put(V_GQL, "q_latent_norm")
    put(V_GKVL, "kv_latent_norm")
    put(V_GQ, "mla_q_norm")
    put(V_GK, "mla_k_norm")
    put(V_GMO, "mla_out_norm")
    put(V_GDQ, "diff_q_norm")
    put(V_GDK, "diff_k_norm")
    put(V_LQ1, "lambda_q1")
    put(V_LK1, "lambda_k1")
    put(V_LQ2, "lambda_q2")
    put(V_LK2, "lambda_k2")
    put(V_GDO, "diff_out_norm")
    vecs = np.ascontiguousarray(np.broadcast_to(vec[None, :], (128, NVEC)))
    gffn = np.ascontiguousarray(np.broadcast_to(np.asarray(inp["ffn_norm"], f)[0][None, :], (128, D)))
    gcol = np.ascontiguousarray(np.asarray(inp["attn_norm"], f)[0].reshape(16, 128).T)
    wlist = {"wa": wa, "wd": wd.reshape(D, NH * 384), "wqup": np.asarray(inp["w_q_up"], f)[0],
             "wkvup": np.asarray(inp["w_kv_up"], f)[0], "wo": np.asarray(inp["w_o"], f)[0],
             "wg": np.asarray(inp["w_gate"], f)[0], "wu": np.asarray(inp["w_up"], f)[0],
             "wdn": np.asarray(inp["w_down"], f)[0]}
    wts = np.empty(WTS_TOTAL, f)
    for n_, (off, r, c) in WTS_OFF.items():
        assert wlist[n_].shape == (r, c), (n_, wlist[n_].shape)
        wts[off:off + r * c] = wlist[n_].reshape(-1)
    shared = {"wts": wts}
    pos = np.arange(S, dtype=np.float64)
    fm = 1.0 / (500000.0 ** (np.arange(0, 64, 2, dtype=np.float64) / 64))
    fd = 1.0 / (500000.0 ** (np.arange(0, 16, 2, dtype=np.float64) / 16))
    am = pos[:, None] * fm[None, :]
    ad = pos[:, None] * fd[None, :]
    rope_full = np.concatenate([np.cos(am), np.sin(am), np.cos(ad), np.sin(ad)], axis=1).astype(f)
    tri = (np.arange(128)[:, None] <= np.arange(128)[None, :]).astype(f)
    in_maps = []
    perms = []
    for c in range(8):
        b, p = c // 2, c % 2
        own, oth = (G0, G1) if p == 0 else (G1, G0)
        perm = np.concatenate([np.arange(g * 128, (g + 1) * 128) for g in own + oth])
        perms.append((b, perm[:1024]))
        xp = x[b][perm]
        masks = np.zeros((128, 3, 128), f)
        masks[:, 0, :] = tri
        masks[:, 1, :] = 1.0 if p == 1 else 0.0
        masks[:, 2, :] = 1.0 if p == 0 else 0.0
        consts = np.zeros((128, NCONST), f)
        consts[:, C_VECS:C_VECS + NVEC] = vecs
        consts[:, C_GFFN:C_GFFN + D] = gffn
        consts[:, C_GCOL:C_GCOL + 16] = gcol
        consts[:, C_MASK:C_MASK + 384] = masks.reshape(128, 384)
        consts[:, C_IDENT:C_IDENT + 128] = np.eye(128, dtype=f)
        consts[:, C_ROPE:] = rope_full[perm].reshape(16, 128, NROPE).transpose(1, 0, 2).reshape(128, 16 * NROPE)
        m = dict(shared)
        m["xT"] = np.ascontiguousarray(xp.T)
        m["xtm"] = np.ascontiguousarray(xp)
        m["consts"] = consts
        in_maps.append(m)
    return in_maps, perms


_NC_CACHE = {}


def kernel(**inputs):
    in_maps, perms = _prep_inputs(inputs)
    if "full" not in _NC_CACHE:
        _NC_CACHE["full"] = build_nc("full")
    nc = _NC_CACHE["full"]
    res = run_bass_kernel_spmd(nc, in_maps, core_ids=list(range(8)))
    out = np.zeros((4, S, D), np.float32)
    for c in range(8):
        b, rows = perms[c]
        out[b, rows, :] = res.results[c]["out"]
    return out
```

```python
import math
from contextlib import ExitStack

import numpy as np
import concourse.bass as bass
import concourse.mybir as mybir
from concourse.bass_utils import run_bass_kernel_spmd

F32 = mybir.dt.float32
BF16 = mybir.dt.bfloat16
AF = mybir.ActivationFunctionType
ALU = mybir.AluOpType
AX = mybir.AxisListType

D = 2048
S = 2048
NH = 8
DFF = 5632
NFC = DFF // 128
EPS = 1e-6
LAMBDA_INIT = 0.8 - 0.6 * math.exp(-0.3 * 0)
G0 = [0, 3, 4, 7, 8, 11, 12, 15]
G1 = [1, 2, 5, 6, 9, 10, 13, 14]

V_GQL, V_GKVL, V_GQ, V_GK, V_GMO, V_GDQ, V_GDK, V_LQ1, V_LK1, V_LQ2, V_LK2, V_GDO = (
    0, 512, 1024, 1216, 1408, 1536, 1600, 1664, 1728, 1792, 1856, 1920)
NVEC = 2048
R_CM, R_SM, R_CD, R_SD = 0, 32, 64, 72
NROPE = 80
C_VECS, C_GFFN, C_GCOL, C_MASK, C_IDENT, C_ROPE = 0, 2048, 4096, 4112, 4496, 4624
NCONST = C_ROPE + 16 * NROPE
_WSHAPES = [("wa", 2048, 1088), ("wd", 2048, 8 * 384), ("wqup", 512, 1536), ("wkvup", 512, 2048),
            ("wo", 2048, 2048), ("wg", 2048, 5632), ("wu", 2048, 5632), ("wdn", 5632, 2048)]
WTS_OFF = {}
_o = 0
for _n, _r, _c in _WSHAPES:
    WTS_OFF[_n] = (_o, _r, _c)
    _o += _r * _c
WTS_TOTAL = _o


class Slot:
    __slots__ = ("name", "w", "r", "excl")

    def __init__(self, name="", excl=False):
        self.name = name
        self.w = None
        self.r = {}
        self.excl = excl


class Prog:
    ENGS = ("pe", "act", "dve", "pool", "sp")

    def __init__(self, nc, es, ring=16):
        self.nc = nc
        self.lists = {e: [] for e in self.ENGS}
        self.sems = {}
        self.cnt = {}
        self.seen = {e: {} for e in self.ENGS}
        for e in ("pe", "act", "dve", "pool"):
            self.sems[e] = es.enter_context(nc.semaphore("c_" + e))
            self.cnt[e] = 0
        self.rings = {}
        self.ridx = {}
        for q in ("sp", "pool"):
            names = []
            for i in range(ring):
                n = "d_%s%d" % (q, i)
                self.sems[n] = es.enter_context(nc.semaphore(n))
                self.cnt[n] = 0
                names.append(n)
            self.rings[q] = names
            self.ridx[q] = 0

    skip = False
    defer = None

    def deferred(self, f):
        self.defer = []
        f()
        L = self.defer
        self.defer = None
        return L

    def release(self, L, n=None):
        k = len(L) if n is None else min(n, len(L))
        for _ in range(k):
            a = L.pop(0)
            self.emit(*a)

    def emit(self, eng, fn, reads=(), writes=(), dma=False):
        if self.skip:
            return None
        if self.defer is not None:
            self.defer.append((eng, fn, list(reads), list(writes), dma))
            return None
        ex = [x for x in reads if x.excl]
        if ex:
            reads = [x for x in reads if not x.excl]
            writes = list(writes) + ex
        deps = []
        for s in reads:
            if s.w is not None:
                deps.append(s.w)
        for s in writes:
            if s.w is not None:
                deps.append(s.w)
            deps.extend(s.r.items())
        if dma:
            ring = self.rings[eng]
            i = self.ridx[eng]
            self.ridx[eng] = (i + 1) % len(ring)
            sem = ring[i]
            if self.cnt[sem] > 0:
                deps.append((sem, self.cnt[sem]))
            inc = 16
        else:
            sem = eng
            inc = 1
        val = self.cnt[sem] + inc
        self.cnt[sem] = val
        need = {}
        for (s, v) in deps:
            if eng == "pe" and s == "pe":
                continue
            if self.seen[eng].get(s, 0) >= v:
                continue
            if need.get(s, 0) < v:
                need[s] = v
        for s, v in need.items():
            self.seen[eng][s] = v
        self.lists[eng].append((list(need.items()), fn, sem, inc))
        ev = (sem, val)
        for s in reads:
            if s.r.get(sem, 0) < val:
                s.r[sem] = val
        for s in writes:
            s.w = ev
            s.r = {}
        return ev

    def barrier(self, engines=None):
        if self.skip:
            return
        for e in (engines or self.ENGS):
            need = []
            for s, v in self.cnt.items():
                if v > 0 and self.seen[e].get(s, 0) < v:
                    if e == "pe" and s == "pe":
                        continue
                    need.append((s, v))
                    self.seen[e][s] = v
            if need:
                self.lists[e].append((need, None, None, 0))

    def replay(self, block):
        def run(e, L):
            for waits, fn, sem, inc in L:
                for s, v in waits:
                    e.wait_ge(self.sems[s], v)
                if fn is not None:
                    ins = fn(e)
                    ins.then_inc(self.sems[sem], inc)

        @block.tensor
        def _(e):
            run(e, self.lists["pe"])

        @block.scalar
        def _(e):
            run(e, self.lists["act"])

        @block.vector
        def _(e):
            run(e, self.lists["dve"])

        @block.gpsimd
        def _(e):
            run(e, self.lists["pool"])

        @block.sync
        def _(e):
            run(e, self.lists["sp"])


class Arena:
    def __init__(self, tensor, nbytes):
        self.t = tensor
        self.nbytes = nbytes
        self.top = 0
        self.peak = 0

    def alloc(self, shape, dtype, at=None):
        esz = 4 if dtype == F32 else 2
        n = 1
        for s in shape[1:]:
            n *= s
        nb = (n * esz + 63) // 64 * 64
        if at is None:
            off = self.top
            self.top += nb
        else:
            off = at
        assert off + nb <= self.nbytes, ("arena overflow", off, nb, self.nbytes)
        self.peak = max(self.peak, off + nb)
        ap = self.t[:, off // 2:(off + n * esz) // 2]
        if dtype == F32:
            ap = ap.bitcast(F32)
        if len(shape) == 3:
            ap = ap.rearrange("p (a b) -> p a b", a=shape[1])
        elif len(shape) == 4:
            ap = ap.rearrange("p (a b c) -> p a b c", a=shape[1], b=shape[2])
        if shape[0] < 128:
            ap = ap[0:shape[0]]
        return ap


def bc(ap, shape):
    return ap.to_broadcast(list(shape))


def build_nc(stage="full"):
    nc = bass.Bass("TRN2", target_bir_lowering=False)
    dram = lambda n, sh, kind="ExternalInput": nc.dram_tensor(n, list(sh), F32, kind=kind).ap()
    xT_d = dram("xT", [D, S])
    xtm_d = dram("xtm", [S, D])
    consts_d = dram("consts", [128, NCONST])
    vecs_d = consts_d[:, C_VECS:C_VECS + NVEC]
    gffn_d = consts_d[:, C_GFFN:C_GFFN + D]
    gcol_d = consts_d[:, C_GCOL:C_GCOL + 16]
    masks_d = consts_d[:, C_MASK:C_MASK + 384].rearrange("p (a b) -> p a b", a=3)
    ident_d = consts_d[:, C_IDENT:C_IDENT + 128]
    rope_d = consts_d[:, C_ROPE:C_ROPE + 16 * NROPE].rearrange("p (i c) -> p i c", i=16)
    wts_d = nc.dram_tensor("wts", [WTS_TOTAL], F32, kind="ExternalInput").ap()

    def wview(name):
        off, r, c = WTS_OFF[name]
        return wts_d[off:off + r * c].rearrange("(r c) -> r c", c=c)

    wa_d = wview("wa")
    wd_d = wview("wd").rearrange("r (h n) -> r h n", h=NH)
    wqup_d = wview("wqup")
    wkvup_d = wview("wkvup")
    wo_d = wview("wo")
    wg_d = wview("wg")
    wu_d = wview("wu")
    wdn_d = wview("wdn")
    out_d = dram("out", [1024, D], kind="ExternalOutput")

    es = ExitStack()
    with es:
        ARENA_BYTES = 204 * 1024
        arena_t = es.enter_context(nc.sbuf_tensor("arena", [128, ARENA_BYTES // 2], BF16))
        ps_t = es.enter_context(nc.psum_tensor("ps", [128, 8, 512], F32))
        P = Prog(nc, es)
        A = Arena(arena_t, ARENA_BYTES)
        KB = 1024

        psf = [ps_t[:, b, :] for b in range(8)]
        psb = [ps_t[:, b, :].bitcast(BF16) for b in range(8)]
        bank = [Slot("bank%d" % b, excl=True) for b in range(8)]
        out_slot = Slot("out")

        vecs = A.alloc([128, NVEC], F32)
        ropet = A.alloc([128, 16, NROPE], F32)
        masks = A.alloc([128, 3, 128], BF16)
        ident = A.alloc([128, 128], BF16)
        gcol = A.alloc([128, 16], F32)
        ssx = A.alloc([128, 16], F32)
        rx = A.alloc([128, 16], F32)
        lamc = A.alloc([128, 8], F32)
        small = A.alloc([128, 256], F32)
        P_TOP = A.top
        OCAT_OFF = P_TOP
        ocat = A.alloc([128, 8, D], BF16)
        BASE = A.top
        s_const = Slot("const")
        s_ssx = Slot("ssx")
        s_rx = Slot("rx")
        s_lam = Slot("lam")
        s_small = Slot("small")
        s_ocat = Slot("ocat")

        def dma(q, out, in_, reads=(), writes=()):
            return P.emit(q, lambda e: e.dma_start(out=out, in_=in_), reads, writes, dma=True)

        def act(out, in_, func, reads, writes, scale=1.0, bias=0.0, accum_out=None):
            kw = {}
            if accum_out is not None:
                kw["accum_out"] = accum_out
            if func == AF.Copy:
                return P.emit("act", lambda e: e.activation(out=out, in_=in_, func=func, scale=scale, **kw), reads, writes)
            return P.emit("act", lambda e: e.activation(out=out, in_=in_, func=func, scale=scale, bias=bias, **kw),
                          reads, writes)

        def tt(out, in0, in1, op, reads, writes, eng="dve"):
            return P.emit(eng, lambda e: e.tensor_tensor(out=out, in0=in0, in1=in1, op=op), reads, writes)

        def ts(out, in0, s1, op0, reads, writes, s2=None, op1=None, eng="dve"):
            if op1 is None:
                return P.emit(eng, lambda e: e.tensor_scalar(out=out, in0=in0, scalar1=s1, scalar2=None, op0=op0),
                              reads, writes)
            return P.emit(eng, lambda e: e.tensor_scalar(out=out, in0=in0, scalar1=s1, scalar2=s2, op0=op0, op1=op1),
                          reads, writes)

        def cp(out, in_, reads, writes, eng="dve"):
            if eng == "act":
                return act(out, in_, AF.Copy, reads, writes)
            return P.emit(eng, lambda e: e.tensor_copy(out=out, in_=in_), reads, writes)

        def red(out, in_, reads, writes):
            return P.emit("dve", lambda e: e.tensor_reduce(out=out, in_=in_, axis=AX.X, op=ALU.add), reads, writes)

        def rsqrt(out, in_, scale, reads, writes, tmp, tmp_slot):
            act(tmp, in_, AF.Ln, reads, [tmp_slot], scale=scale, bias=EPS)
            act(out, tmp, AF.Exp, [tmp_slot], writes, scale=-0.5)

        def mm_group(mms, reads, writes):
            def fn(e):
                ins = None
                for m in mms:
                    kw = {}
                    if len(m) > 5 and m[5]:
                        kw["skip_group_check"] = True
                    ins = e.matmul(m[0], lhsT=m[1], rhs=m[2], start=m[3], stop=m[4], **kw)
                return ins
            return P.emit("pe", fn, reads, writes)

        def tr_group(trs, reads, writes):
            def fn(e):
                ins = None
                for (o, i) in trs:
                    ins = e.transpose(o, i, ident)
                return ins
            return P.emit("pe", fn, list(reads) + [s_const], writes)

        if stage != "full":
            P.emit("pool", lambda e: e.memset(ocat, 0.0), [], [s_ocat])
        dma("sp", vecs, vecs_d, writes=[s_const])
        dma("sp", ropet, rope_d, writes=[s_const])
        dma("sp", gcol, gcol_d, writes=[s_const])
        dma("pool", masks, masks_d, writes=[s_const])
        dma("pool", ident, ident_d, writes=[s_const])

        sm = small
        tt(sm[:, 0:64], vecs[:, V_LQ1:V_LQ1 + 64], vecs[:, V_LK1:V_LK1 + 64], ALU.mult, [s_const], [s_small])
        tt(sm[:, 64:128], vecs[:, V_LQ2:V_LQ2 + 64], vecs[:, V_LK2:V_LK2 + 64], ALU.mult, [s_const], [s_small])
        red(sm[:, 128:130], sm[:, 0:128].rearrange("p (a b) -> p a b", a=2), [s_small], [s_small])
        act(sm[:, 130:132], sm[:, 128:130], AF.Exp, [s_small], [s_small])
        tt(lamc[:, 0:1], sm[:, 130:131], sm[:, 131:132], ALU.subtract, [s_small], [s_lam])
        ts(lamc[:, 0:1], lamc[:, 0:1], LAMBDA_INIT, ALU.add, [s_lam], [s_lam])
        ts(lamc[:, 1:2], lamc[:, 0:1], -1.0, ALU.mult, [s_lam], [s_lam])

        if stage == "p0":
            P.skip = True
        s_ssxg = [Slot() for _ in range(4)]
        s_rxg = [Slot() for _ in range(4)]

        def srx(i):
            return s_rxg[i // 4]

        A.top = BASE
        cqnT = A.alloc([128, 4, 1024], BF16)
        ckvnT = A.alloc([128, 4, S], BF16)
        kr_tm = A.alloc([128, 16, 64], F32)
        sspe = A.alloc([128, 16], F32)
        LAT_TOP = A.top
        s_cqnT = Slot("cqnT")
        s_ckvnT = Slot("ckvnT")
        s_kr = Slot("kr")
        s_sspe = Slot("sspe")

        xTg = [A.alloc([128, 16, 512], BF16) for _ in range(2)]
        sqT = A.alloc([128, 16, 512], BF16)
        ones2 = A.alloc([128, 2], BF16)
        s_sqT = Slot()
        s_ones = Slot()
        P.emit("dve", lambda e: e.memset(ones2, 1.0), [], [s_ones])

        def stats_group(g, xb, sxb):
            for c in range(16):
                act(sqT[:, c, :], xb[:, c, :], AF.Square, [sxb], [s_sqT])
            b_ = next_bank()
            for tl in range(4):
                mm_group([(psf[b_][:, 2 * tl:2 * tl + 2], sqT[:, c, tl * 128:(tl + 1) * 128], ones2, c == 0, c == 15)
                          for c in range(16)], [s_sqT, s_ones], [bank[b_]])
            cp(ssx[:, g * 4:g * 4 + 4], psf[b_][:, 0:8].rearrange("p (t k) -> p t k", k=2)[:, :, 0], [bank[b_]], [s_ssxg[g]])
            rsqrt(rx[:, g * 4:g * 4 + 4], ssx[:, g * 4:g * 4 + 4], 1.0 / D, [s_ssxg[g]], [s_rxg[g]],
                  sm[:, 200 + g * 4:204 + g * 4], s_small)
        wql = A.alloc([128, 16, 512], BF16)
        wkvl = A.alloc([128, 16, 512], BF16)
        wpe = A.alloc([128, 16, 64], BF16)
        cf = [A.alloc([128, 4, 512], F32) for _ in range(2)]
        cbf = A.alloc([128, 512], BF16)
        sq = A.alloc([128, 2048], F32)
        kpe_tm = A.alloc([128, 16, 64], F32)
        rtmp = A.alloc([128, 16, 64], F32)
        s_xTg = [Slot("xTg0"), Slot("xTg1")]
        s_cf = [Slot("cf0"), Slot("cf1")]
        s_cbf = Slot("cbf")
        s_sq = Slot("sq")
        s_kpe = Slot("kpe")
        s_rtmp = Slot("rtmp")

        wa_v = wa_d.rearrange("(c p) n -> p c n", p=128)
        xT_v = xT_d.rearrange("(c p) t -> p c t", p=128)
        s_wkvl = Slot()
        s_wql = Slot()
        s_wpe = Slot()
        dma("pool", wkvl, wa_v[:, :, 512:1024], writes=[s_wkvl])

        def load_xT_group(dst, s_dst, t0, n):
            dma("pool", dst, xT_v[:, :, t0:t0 + n], writes=[s_dst])

        def scale_xT(dst, s_dst, n):
            for c in range(16):
                ts(dst[:, c, 0:n], dst[:, c, 0:n], gcol[:, c:c + 1], ALU.mult, [s_const, s_dst], [s_dst])

        bk = [0]

        def next_bank():
            b = bk[0]
            bk[0] = (b + 1) % 8
            return b

        load_xT_group(xTg[0], s_xTg[0], 0, 512)
        dma("pool", wpe, wa_v[:, :, 1024:1088], writes=[s_wpe])
        load_xT_group(xTg[1], s_xTg[1], 512, 512)
        dma("pool", wql, wa_v[:, :, 0:512], writes=[s_wql])
        units = [(0, "kv"), (0, "q"), (1, "kv"), (1, "q"), (2, "kv"), (3, "kv")]
        prepared = set()

        def prep_group(tg):
            if tg in prepared or tg >= 4:
                return
            prepared.add(tg)
            xb = xTg[tg % 2]
            sxb = s_xTg[tg % 2]
            stats_group(tg, xb, sxb)
            scale_xT(xb, sxb, 512)

        last_unit_of_group = {1: 0, 3: 1, 4: 2, 5: 3}

        def unit_mm(u):
            tg, kind = units[u]
            prep_group(tg)
            xb = xTg[tg % 2]
            sxb = s_xTg[tg % 2]
            W = wkvl if kind == "kv" else wql
            s_wl = s_wkvl if kind == "kv" else s_wql
            cfi = u % 2
            for tl in range(4):
                i = tg * 4 + tl
                b = next_bank()
                mm_group([(psf[b], xb[:, c, tl * 128:(tl + 1) * 128], W[:, c, :], c == 0, c == 15) for c in range(16)],
                         [sxb, s_wl], [bank[b]])
                act(cf[cfi][:, tl, :], psf[b], AF.Copy, [bank[b], srx(i)], [s_cf[cfi]], scale=rx[:, i:i + 1])
                if kind == "kv":
                    b2 = next_bank()
                    mm_group([(psf[b2][:, 0:64], xb[:, c, tl * 128:(tl + 1) * 128], wpe[:, c, :], c == 0, c == 15)
                              for c in range(16)], [sxb, s_wpe], [bank[b2]])
                    act(kpe_tm[:, i, :], psf[b2][:, 0:64], AF.Copy, [bank[b2], srx(i)], [s_kpe], scale=rx[:, i:i + 1])
            if u in last_unit_of_group:
                g_ = last_unit_of_group[u]
                if g_ + 2 < 4:
                    load_xT_group(xTg[g_ % 2], s_xTg[g_ % 2], (g_ + 2) * 512, 512)
            prep_group(tg + 1)

        def unit_post(u):
            tg, kind = units[u]
            cfi = u % 2
            tt(sq.rearrange("p (a b) -> p a b", a=4), cf[cfi], cf[cfi], ALU.mult, [s_cf[cfi]], [s_sq])
            red(sm[:, 16:20], sq.rearrange("p (a b) -> p a b", a=4), [s_sq], [s_small])
            rsqrt(sm[:, 20:24], sm[:, 16:20], 1.0 / 512, [s_small], [s_small], sm[:, 24:28], s_small)
            gv = vecs[:, V_GKVL:V_GKVL + 512] if kind == "kv" else vecs[:, V_GQL:V_GQL + 512]
            dstT = ckvnT if kind == "kv" else cqnT
            s_dstT = s_ckvnT if kind == "kv" else s_cqnT
            for tl in range(4):
                i = tg * 4 + tl
                P.emit("dve", lambda e, tl=tl, cfi=cfi, gv=gv: e.scalar_tensor_tensor(
                    out=cbf, in0=cf[cfi][:, tl, :], scalar=sm[:, 20 + tl:21 + tl], in1=gv,
                    op0=ALU.mult, op1=ALU.mult), [s_cf[cfi], s_small, s_const], [s_cbf])
                b = next_bank()
                tr_group([(psb[b][:, c * 128:(c + 1) * 128], cbf[:, c * 128:(c + 1) * 128]) for c in range(4)],
                         [s_cbf], [bank[b]])
                cp(dstT[:, :, i * 128:(i + 1) * 128], psb[b][:, 0:512].rearrange("p (c t) -> p c t", c=4),
                   [bank[b]], [s_dstT])

        unit_mm(0)
        for u in range(len(units)):
            if u + 1 < len(units):
                unit_mm(u + 1)
            unit_post(u)

        tt(sq[:, 0:1024].rearrange("p (a b) -> p a b", a=16), kpe_tm, kpe_tm, ALU.mult, [s_kpe], [s_sq])
        red(sspe, sq[:, 0:1024].rearrange("p (a b) -> p a b", a=16), [s_sq], [s_sspe])
        gk_r = vecs[:, V_GK + 128:V_GK + 192]
        tt(kpe_tm, kpe_tm, bc(gk_r.unsqueeze(1), [128, 16, 64]), ALU.mult, [s_kpe, s_const], [s_kpe])
        cosm = ropet[:, :, R_CM:R_CM + 32]
        sinm = ropet[:, :, R_SM:R_SM + 32]

        def rope_tm(dst, s_dstl, src, s_srcl, tmp, s_tmpl, ntile, cos, sin, half):
            x1 = src[:, 0:ntile, 0:half]
            x2 = src[:, 0:ntile, half:2 * half]
            ta = tmp[:, 0:ntile, 0:half]
            tb = tmp[:, 0:ntile, half:2 * half]
            c_ = cos[:, 0:ntile, :]
            s_ = sin[:, 0:ntile, :]
            tt(ta, x1, c_, ALU.mult, s_srcl + [s_const], s_tmpl)
            tt(tb, x2, s_, ALU.mult, s_srcl + [s_const], s_tmpl)
            tt(dst[:, 0:ntile, 0:half], ta, tb, ALU.subtract, s_tmpl, s_dstl)
            tt(ta, x2, c_, ALU.mult, s_srcl + [s_const], s_tmpl)
            tt(tb, x1, s_, ALU.mult, s_srcl + [s_const], s_tmpl)
            tt(dst[:, 0:ntile, half:2 * half], ta, tb, ALU.add, s_tmpl, s_dstl)

        rope_tm(kr_tm, [s_kr], kpe_tm, [s_kpe], rtmp, [s_rtmp], 16, cosm, sinm, 32)
        P.barrier()
        if stage == "p2":
            P.skip = True

        _ptd = {}

        def ptd(slot):
            return _ptd.setdefault(id(slot), Slot())

        def attention(parts, vx, s_in, scale, oacc_banks, st_banks, PT, s_PT, fill=None, nfill=0):
            steps = [(jp, w) for jp in range(8) for w in (0, 1)]
            oloc = []
            for j in range(8):
                bnk = oacc_banks[j // 3]
                oloc.append((bnk, (j % 3) * 129))
            first_in_bank = {}
            last_in_bank = {}
            for s_i, (jp, w) in enumerate(steps):
                for j in list(range(jp + 1, 8)) + [jp]:
                    bnk = oloc[j][0]
                    first_in_bank.setdefault(bnk, (s_i, j))
                    last_in_bank[bnk] = (s_i, j)

            def qk(s_i):
                jp, w = steps[s_i]
                kt = jp + 8 * w
                sb = st_banks[s_i % 2]
                q0 = jp * 128
                mms = []
                for c0 in range(q0, 1024, 512):
                    n = min(512, 1024 - c0)
                    bnk = sb[(c0 - q0) // 512]
                    for pi, (Kp, Qp) in enumerate(parts):
                        mms.append((psf[bnk][:, 0:n], Kp[:, kt * 128:(kt + 1) * 128], Qp[:, c0:c0 + n],
                                    pi == 0, pi == len(parts) - 1))
                mm_group(mms, s_in, [bank[sb[0]], bank[sb[1]]])
                Wd = 1024 - q0
                pb = s_i % len(PT)
                sd = ptd(s_PT[pb])
                if Wd > 512:
                    act(PT[pb][:, 0:512], psf[sb[0]], AF.Exp, [bank[sb[0]]], [s_PT[pb], sd], scale=scale)
                    act(PT[pb][:, 512:Wd], psf[sb[1]][:, 0:Wd - 512], AF.Exp, [bank[sb[1]]], [s_PT[pb]], scale=scale)
                else:
                    act(PT[pb][:, 0:Wd], psf[sb[0]][:, 0:Wd], AF.Exp, [bank[sb[0]]], [s_PT[pb], sd], scale=scale)
                mi = 0 if w == 0 else 1 + (jp % 2)
                tt(PT[pb][:, 0:128], PT[pb][:, 0:128], masks[:, mi, :], ALU.mult, [sd, s_const], [sd])

            def pv(s_i):
                jp, w = steps[s_i]
                kt = jp + 8 * w
                pb = s_i % len(PT)
                for js, slot in ((list(range(jp + 1, 8)), s_PT[pb]), ([jp], ptd(s_PT[pb]))):
                    if not js:
                        continue
                    mms = []
                    wr = set()
                    for j in js:
                        bnk, off = oloc[j]
                        wr.add(bnk)
                        mms.append((psf[bnk][:, off:off + 129], PT[pb][:, (j - jp) * 128:(j - jp + 1) * 128],
                                    vx[:, kt, 0:129], first_in_bank[bnk] == (s_i, j), last_in_bank[bnk] == (s_i, j), True))
                    mm_group(mms, [slot] + list(s_in), [bank[b_] for b_ in sorted(wr)])

            qk(0)
            for s_i in range(16):
                if s_i + 1 < 16:
                    qk(s_i + 1)
                pv(s_i)
                if fill:
                    P.release(fill, nfill)
            return oloc

        def evac_o(oacc_banks, o_f, s_of):
            for bi, bnk in enumerate(oacc_banks):
                nj = 3 if bi < 2 else 2
                cp(o_f[:, bi * 3:bi * 3 + nj, :], psf[bnk][:, 0:nj * 129].rearrange("p (a b) -> p a b", a=nj),
                   [bank[bnk]], [s_of], eng=("dve" if bi != 1 else "act"))

        ST_BANKS = [(0, 1), (2, 3)]
        OACC = [4, 5, 6]

        A.top = LAT_TOP
        knT = [A.alloc([128, S], BF16) for _ in range(2)]
        krT = [A.alloc([128, S], BF16) for _ in range(2)]
        vx = [A.alloc([128, 16, 130], BF16) for _ in range(2)]
        qnT = [A.alloc([128, 1024], BF16) for _ in range(2)]
        qrT = [A.alloc([128, 1024], BF16) for _ in range(2)]
        wkvh = [A.alloc([128, 4, 256], BF16) for _ in range(2)]
        wq_all = A.alloc([128, 4, 1536], BF16)
        wqh = [wq_all[:, :, h_ * 192:(h_ + 1) * 192] for h_ in range(NH)]
        knope_tm = A.alloc([128, 16, 128], F32)
        q_tm = A.alloc([128, 8, 192], F32)
        kn_bf = A.alloc([128, 16, 128], BF16)
        krh_bf = A.alloc([128, 16, 64], BF16)
        qn_bf = A.alloc([128, 8, 128], BF16)
        qr_bf = A.alloc([128, 8, 64], BF16)
        qg = A.alloc([128, 8, 64], F32)
        qtmp = A.alloc([128, 8, 64], F32)
        sq = A.alloc([128, 2048], F32)
        PT = [A.alloc([128, 1024], BF16) for _ in range(3)]
        o_f = A.alloc([128, 8, 129], F32)
        s_knT = [Slot(), Slot()]
        s_krT = [Slot(), Slot()]
        s_vx = [Slot(), Slot()]
        s_qnT = [Slot(), Slot()]
        s_qrT = [Slot(), Slot()]
        s_wkvh = [Slot(), Slot()]
        s_wqh = [Slot(), Slot()]
        s_wqall = Slot()
        s_knope = Slot()
        s_qtm = Slot()
        s_knbf = Slot()
        s_krh = Slot()
        s_qnbf = Slot()
        s_qrbf = Slot()
        s_qg = Slot()
        s_qtmp = Slot()
        s_sq = Slot()
        s_PT = [Slot(), Slot(), Slot()]
        s_of = Slot()

        import os
        VAR = os.environ.get("KVAR", "")
        for hb in range(2):
            P.emit("pool", lambda e, hb=hb: e.memset(krT[hb][64:128, :], 0.0), [], [s_krT[hb]])
            P.emit("pool", lambda e, hb=hb: e.memset(qrT[hb][64:128, :], 0.0), [], [s_qrT[hb]])
            if VAR != "nomemset":
                P.emit("dve" if VAR == "dvememset" else "pool", lambda e, hb=hb: e.memset(vx[hb][:, :, 128:130], 1.0), [], [s_vx[hb]])

        wkvup_v = wkvup_d.rearrange("(c p) n -> p c n", p=128)
        wqup_v = wqup_d.rearrange("(c p) n -> p c n", p=128)

        def load_mla_w(h):
            hb = h % 2
            if VAR == "nodma":
                return
            if VAR != "onlyq":
                dma("pool", wkvh[hb], wkvup_v[:, :, h * 256:(h + 1) * 256], writes=[s_wkvh[hb], s_ord])

        gk_n = vecs[:, V_GK:V_GK + 128]
        gq_n = vecs[:, V_GQ:V_GQ + 128]
        gq_r = vecs[:, V_GQ + 128:V_GQ + 192]
        gmo = vecs[:, V_GMO:V_GMO + 128]

        s_ord = Slot("dmaorder")
        if VAR == "e1":
            dma("pool", wq_all[:, :, 0:128], wkvup_v[:, :, 256:384], writes=[s_wqall, s_ord])
        elif VAR == "e2":
            dma("sp", wq_all.bitcast(F32), wqup_v[:, :, 0:768], writes=[s_wqall, s_ord])
        else:
            dma("pool", wq_all, wqup_v, writes=[s_wqall, s_ord])
        n_mla = NH if stage not in ("mla1", "mla1a", "mla1b", "mla1p1", "mla1p2", "mla1p0", "mla1pk") else 1
        def mla_proj(h):
            hb = h % 2
            for ip in range(8):
                b = next_bank()
                for t in range(2):
                    i = ip * 2 + t
                    mm_group([(psf[b][:, t * 256:(t + 1) * 256], ckvnT[:, c, i * 128:(i + 1) * 128], wkvh[hb][:, c, :],
                               c == 0, c == 3) for c in range(4)], [s_ckvnT, s_wkvh[hb]], [bank[b]])
                pv_ = psf[b].rearrange("p (t c) -> p t c", t=2)
                if VAR not in ("noevac", "noact"):
                    cp(knope_tm[:, ip * 2:ip * 2 + 2, :], pv_[:, :, 0:128], [bank[b]], [s_knope], eng="act")
                if VAR not in ("noevac", "nodve"):
                    cp(vx[hb][:, ip * 2:ip * 2 + 2, 0:128], pv_[:, :, 128:256], [bank[b]], [s_vx[hb]], eng="dve")
            if stage == "mla1pk":
                P.skip = True
            for ip in range(4):
                b = next_bank()
                for t in range(2):
                    i = ip * 2 + t
                    mm_group([(psf[b][:, t * 192:(t + 1) * 192], cqnT[:, c, i * 128:(i + 1) * 128], wqh[h][:, c, :],
                               c == 0, c == 3) for c in range(4)], [s_cqnT, s_wqall], [bank[b]])
                if VAR not in ("noevac", "noact"):
                    cp(q_tm[:, ip * 2:ip * 2 + 2, :], psf[b][:, 0:384].rearrange("p (t c) -> p t c", t=2),
                       [bank[b]], [s_qtm], eng="act")

        def mla_norm(h):
            hb = h % 2
            if stage == "mla1p1":
                P.skip = True
            sq3 = sq.rearrange("p (a b) -> p a b", a=16)
            tt(sq3, knope_tm, knope_tm, ALU.mult, [s_knope], [s_sq])
            red(sm[:, 32:48], sq3, [s_sq], [s_small])
            tt(sm[:, 32:48], sm[:, 32:48], sspe, ALU.add, [s_small, s_sspe], [s_small])
            rsqrt(sm[:, 48:64], sm[:, 32:48], 1.0 / 192, [s_small], [s_small], sm[:, 64:80], s_small)
            rk = sm[:, 48:64]
            tt(knope_tm, knope_tm, bc(rk.unsqueeze(2), [128, 16, 128]), ALU.mult, [s_knope, s_small], [s_knope])
            tt(kn_bf, knope_tm, bc(gk_n.unsqueeze(1), [128, 16, 128]), ALU.mult, [s_knope, s_const], [s_knbf])
            tt(krh_bf, kr_tm, bc(rk.unsqueeze(2), [128, 16, 64]), ALU.mult, [s_kr, s_small], [s_krh])
            if stage == "mla1p2":
                P.skip = True
            sq3q = sq[:, 0:1536].rearrange("p (a b) -> p a b", a=8)
            tt(sq3q, q_tm, q_tm, ALU.mult, [s_qtm], [s_sq])
            red(sm[:, 80:88], sq3q, [s_sq], [s_small])
            rsqrt(sm[:, 88:96], sm[:, 80:88], 1.0 / 192, [s_small], [s_small], sm[:, 96:104], s_small)
            rq = sm[:, 88:96]
            tt(q_tm, q_tm, bc(rq.unsqueeze(2), [128, 8, 192]), ALU.mult, [s_qtm, s_small], [s_qtm])
            tt(qn_bf, q_tm[:, :, 0:128], bc(gq_n.unsqueeze(1), [128, 8, 128]), ALU.mult, [s_qtm, s_const], [s_qnbf])
            tt(qg, q_tm[:, :, 128:192], bc(gq_r.unsqueeze(1), [128, 8, 64]), ALU.mult, [s_qtm, s_const], [s_qg])
            rope_tm(qr_bf, [s_qrbf], qg, [s_qg], qtmp, [s_qtmp], 8, cosm, sinm, 32)

        def mla_tr(h):
            hb = h % 2
            for g in range(2):
                b = next_bank()
                tr_group([(psb[b][:, t * 128:(t + 1) * 128], kn_bf[:, g * 8 + t, :]) for t in range(8)],
                         [s_knbf], [bank[b]])
                cp(knT[hb][:, g * 1024:(g + 1) * 1024], psb[b], [bank[b]], [s_knT[hb]], eng=("act" if g else "dve"))
                b = next_bank()
                tr_group([(psb[b][0:64, t * 128:(t + 1) * 128], krh_bf[:, g * 8 + t, :]) for t in range(8)],
                         [s_krh], [bank[b]])
                cp(krT[hb][0:64, g * 1024:(g + 1) * 1024], psb[b][0:64, :], [bank[b]], [s_krT[hb]],
                   eng=("dve" if g else "act"))
            b = next_bank()
            tr_group([(psb[b][:, t * 128:(t + 1) * 128], qn_bf[:, t, :]) for t in range(8)], [s_qnbf], [bank[b]])
            cp(qnT[hb], psb[b], [bank[b]], [s_qnT[hb]], eng="dve")
            b = next_bank()
            tr_group([(psb[b][0:64, t * 128:(t + 1) * 128], qr_bf[:, t, :]) for t in range(8)], [s_qrbf], [bank[b]])
            cp(qrT[hb][0:64, :], psb[b][0:64, :], [bank[b]], [s_qrT[hb]], eng="act")

        def mla_attn(h, fill, nfill):
            hb = h % 2
            if stage == "mla1a":
                P.skip = True
            attention([(knT[hb], qnT[hb]), (krT[hb], qrT[hb])], vx[hb],
                      [s_knT[hb], s_qnT[hb], s_krT[hb], s_qrT[hb], s_vx[hb]], 1.0 / math.sqrt(192.0),
                      OACC, ST_BANKS, PT, s_PT, fill=fill, nfill=nfill)
            evac_o(OACC, o_f, s_of)
            if stage == "mla1b":
                P.skip = True
            P.emit("dve", lambda e: e.reciprocal(out=sm[:, 104:112], in_=o_f[:, :, 128]), [s_of], [s_small])
            sq3o = sq[:, 0:1024].rearrange("p (a b) -> p a b", a=8)
            tt(sq3o, o_f[:, :, 0:128], o_f[:, :, 0:128], ALU.mult, [s_of], [s_sq])
            red(sm[:, 112:120], sq3o, [s_sq], [s_small])
            tt(sm[:, 112:120], sm[:, 112:120], sm[:, 104:112], ALU.mult, [s_small], [s_small])
            tt(sm[:, 112:120], sm[:, 112:120], sm[:, 104:112], ALU.mult, [s_small], [s_small])
            rsqrt(sm[:, 120:128], sm[:, 112:120], 1.0 / 128, [s_small], [s_small], sm[:, 128:136], s_small)
            tt(sm[:, 120:128], sm[:, 120:128], sm[:, 104:112], ALU.mult, [s_small], [s_small])
            tt(o_f[:, :, 0:128], o_f[:, :, 0:128], bc(sm[:, 120:128].unsqueeze(2), [128, 8, 128]), ALU.mult,
               [s_of, s_small], [s_of])
            tt(ocat[:, :, h * 128:(h + 1) * 128], o_f[:, :, 0:128], bc(gmo.unsqueeze(1), [128, 8, 128]), ALU.mult,
               [s_of, s_const], [s_ocat])

        XH_OFF = 172 * 1024
        assert A.top <= XH_OFF, A.top
        NXH = 8
        xTh = A.alloc([128, NXH, S], BF16, at=XH_OFF)
        s_xT = Slot()

        def prefetch_xT_high():
            for tg in range(4):
                dma("pool", xTh[:, :, tg * 512:(tg + 1) * 512], xT_v[:, 0:NXH, tg * 512:(tg + 1) * 512], writes=[s_xT])

        load_mla_w(0)
        if n_mla > 1:
            load_mla_w(1)
        mla_proj(0)
        mla_norm(0)
        mla_tr(0)
        for h in range(n_mla):
            fill = []
            if h + 1 < n_mla:
                mla_proj(h + 1)
                if h + 2 < n_mla:
                    load_mla_w(h + 2)
                fill = P.deferred(lambda: mla_norm(h + 1))
            nfill = (len(fill) + 13) // 14
            if h == n_mla - 2 and stage not in ("mla1", "mla"):
                prefetch_xT_high()
            mla_attn(h, fill, nfill)
            P.release(fill)
            if h + 1 < n_mla:
                mla_tr(h + 1)
        P.barrier()

        if stage not in ("mla1", "mla1a", "mla1b", "mla1p1", "mla1p2", "mla1p0", "mla1pk", "mla"):
            A.top = BASE
            xTl = A.alloc([128, 16 - NXH, S], BF16)

            def xTc(c):
                return xTh[:, c, :] if c < NXH else xTl[:, c - NXH, :]
            wdh = [A.alloc([128, 16, 384], BF16) for _ in range(2)]
            dkT = [A.alloc([128, S], BF16) for _ in range(2)]
            dqT = [[A.alloc([128, 1024], BF16) for _c in range(2)] for _ in range(2)]
            dvx = [A.alloc([128, 16, 130], BF16) for _ in range(2)]
            dk_tm = A.alloc([128, 16, 128], F32)
            dq_tm = A.alloc([128, 8, 128], F32)
            dk_bf = A.alloc([128, 16, 128], BF16)
            dq_bf = A.alloc([128, 8, 128], BF16)
            dtmp = A.alloc([128, 16, 32], F32)
            sq = A.alloc([128, 2048], F32)
            PT = [A.alloc([128, 1024], BF16) for _ in range(3)]
            o1_f = A.alloc([128, 8, 129], F32)
            o2_f = A.alloc([128, 8, 129], F32)
            s_wdh = [Slot(), Slot()]
            s_dkT = [Slot(), Slot()]
            s_dqT = [Slot(), Slot()]
            s_dvx = [Slot(), Slot()]
            s_dktm = Slot()
            s_dqtm = Slot()
            s_dkbf = Slot()
            s_dqbf = Slot()
            s_dtmp = Slot()
            s_sq = Slot()
            s_PT = [Slot(), Slot(), Slot()]
            s_o1 = Slot()
            s_o2 = Slot()
            for hb in range(2):
                P.emit("pool", lambda e, hb=hb: e.memset(dvx[hb][:, :, 128:130], 1.0), [], [s_dvx[hb]])
                P.emit("pool", lambda e, hb=hb: e.memset(dqT[hb][0][64:128, :], 0.0), [], [s_dqT[hb]])
                P.emit("pool", lambda e, hb=hb: e.memset(dqT[hb][1][0:64, :], 0.0), [], [s_dqT[hb]])
            wd_v = wd_d.rearrange("(c p) h n -> p c h n", p=128)

            def load_wd(h):
                dma("pool", wdh[h % 2], wd_v[:, :, h, :], writes=[s_wdh[h % 2]])

            if n_mla < 2:
                prefetch_xT_high()
            for tg in range(4):
                dma("pool", xTl[:, :, tg * 512:(tg + 1) * 512], xT_v[:, NXH:16, tg * 512:(tg + 1) * 512], writes=[s_xT])
            load_wd(0)
            for c in range(16):
                ts(xTc(c), xTc(c), gcol[:, c:c + 1], ALU.mult, [s_const, s_xT], [s_xT])
            gdq = vecs[:, V_GDQ:V_GDQ + 64]
            gdk = vecs[:, V_GDK:V_GDK + 64]
            gdo = vecs[:, V_GDO:V_GDO + 128]
            cosd = ropet[:, :, R_CD:R_CD + 8]
            sind = ropet[:, :, R_SD:R_SD + 8]

            def diff_norm_rope(src, s_src, dst, s_dst, ntile, gain, c0, c1):
                nsub = ntile * 2
                s4 = src.rearrange("p t (c d) -> p (t c) d", c=2)
                sqv = sq[:, 0:nsub * 64].rearrange("p (a b) -> p a b", a=nsub)
                tt(sqv, s4, s4, ALU.mult, [s_src], [s_sq])
                red(sm[:, c0:c0 + nsub], sqv, [s_sq], [s_small])
                rsqrt(sm[:, c1:c1 + nsub], sm[:, c0:c0 + nsub], 1.0 / 64, [s_small], [s_small],
                      sm[:, c0:c0 + nsub], s_small)
                tt(s4, s4, bc(sm[:, c1:c1 + nsub].unsqueeze(2), [128, nsub, 64]), ALU.mult, [s_src, s_small], [s_src])
                tt(s4, s4, bc(gain.unsqueeze(1), [128, nsub, 64]), ALU.mult, [s_src, s_const], [s_src])
                d4 = dst.rearrange("p t (c d) -> p (t c) d", c=2)
                cp(d4[:, :, 16:64], s4[:, :, 16:64], [s_src], [s_dst])
                s5 = src.rearrange("p t (c d) -> p t c d", c=2)
                d5 = dst.rearrange("p t (c d) -> p t c d", c=2)
                cs = bc(cosd[:, 0:ntile, :].unsqueeze(2), [128, ntile, 2, 8])
                sn = bc(sind[:, 0:ntile, :].unsqueeze(2), [128, ntile, 2, 8])
                t5 = dtmp[:, 0:ntile, :].rearrange("p t (c d) -> p t c d", c=2)
                x1 = s5[:, :, :, 0:8]
                x2 = s5[:, :, :, 8:16]
                tt(t5[:, :, :, 0:8], x1, cs, ALU.mult, [s_src, s_const], [s_dtmp])
                tt(t5[:, :, :, 8:16], x2, sn, ALU.mult, [s_src, s_const], [s_dtmp])
                tt(d5[:, :, :, 0:8], t5[:, :, :, 0:8], t5[:, :, :, 8:16], ALU.subtract, [s_dtmp], [s_dst])
                tt(t5[:, :, :, 0:8], x2, cs, ALU.mult, [s_src, s_const], [s_dtmp])
                tt(t5[:, :, :, 8:16], x1, sn, ALU.mult, [s_src, s_const], [s_dtmp])
                tt(d5[:, :, :, 8:16], t5[:, :, :, 0:8], t5[:, :, :, 8:16], ALU.add, [s_dtmp], [s_dst])

            n_diff = NH if stage != "diff1" else 1
            woh = A.alloc([128, NXH, D], BF16, at=XH_OFF)
            wol = A.alloc([128, 16 - NXH, D], BF16, at=BASE)

            def woc(c):
                return woh[:, c, :] if c < NXH else wol[:, c - NXH, :]
            s_wo = [Slot() for _ in range(4)]
            wo_v = wo_d.rearrange("(c p) n -> p c n", p=128)
            def diff_proj(h):
                hb = h % 2
                for i in range(16):
                    b = next_bank()
                    lo = 0 if i < 8 else 128
                    n = 384 - lo
                    mm_group([(psf[b][:, 0:n], xTc(c)[:, i * 128:(i + 1) * 128], wdh[hb][:, c, lo:384], c == 0, c == 15)
                              for c in range(16)], [s_xT, s_wdh[hb]], [bank[b]])
                    o_ = 0
                    if i < 8:
                        act(dq_tm[:, i, :], psf[b][:, 0:128], AF.Copy, [bank[b], srx(i)], [s_dqtm], scale=rx[:, i:i + 1])
                        o_ = 128
                    act(dk_tm[:, i, :], psf[b][:, o_:o_ + 128], AF.Copy, [bank[b], srx(i)], [s_dktm], scale=rx[:, i:i + 1])
                    ts(dvx[hb][:, i, 0:128], psf[b][:, o_ + 128:o_ + 256], rx[:, i:i + 1], ALU.mult,
                       [bank[b], srx(i)], [s_dvx[hb]])

            def diff_norm(h):
                hb = h % 2
                diff_norm_rope(dk_tm, s_dktm, dk_bf, s_dkbf, 16, gdk, 32, 64)
                diff_norm_rope(dq_tm, s_dqtm, dq_bf, s_dqbf, 8, gdq, 96, 112)

            def diff_tr(h):
                hb = h % 2
                for g in range(2):
                    b = next_bank()
                    tr_group([(psb[b][:, t * 128:(t + 1) * 128], dk_bf[:, g * 8 + t, :]) for t in range(8)],
                             [s_dkbf], [bank[b]])
                    cp(dkT[hb][:, g * 1024:(g + 1) * 1024], psb[b], [bank[b]], [s_dkT[hb]], eng=("act" if g else "dve"))
                b = next_bank()
                tr_group([(psb[b][:, t * 128:(t + 1) * 128], dq_bf[:, t, :]) for t in range(8)], [s_dqbf], [bank[b]])
                cp(dqT[hb][0][0:64, :], psb[b][0:64, :], [bank[b]], [s_dqT[hb]], eng="dve")
                cp(dqT[hb][1][64:128, :], psb[b][64:128, :], [bank[b]], [s_dqT[hb]], eng="act")

            def diff_attn(h, fill, nfill):
                hb = h % 2
                for c in range(2):
                    attention([(dkT[hb], dqT[hb][c])], dvx[hb],
                              [s_dkT[hb], s_dqT[hb], s_dvx[hb]], 1.0 / 8.0, OACC, ST_BANKS, PT, s_PT, fill=fill, nfill=nfill)
                    evac_o(OACC, o1_f if c == 0 else o2_f, s_o1 if c == 0 else s_o2)
                P.emit("dve", lambda e: e.reciprocal(out=sm[:, 128:136], in_=o1_f[:, :, 128]), [s_o1], [s_small])
                P.emit("dve", lambda e: e.reciprocal(out=sm[:, 136:144], in_=o2_f[:, :, 128]), [s_o2], [s_small])
                ts(sm[:, 136:144], sm[:, 136:144], lamc[:, 1:2], ALU.mult, [s_small, s_lam], [s_small])
                d1 = o1_f[:, :, 0:128]
                d2 = o2_f[:, :, 0:128]
                tt(d1, d1, bc(sm[:, 128:136].unsqueeze(2), [128, 8, 128]), ALU.mult, [s_o1, s_small], [s_o1])
                tt(d2, d2, bc(sm[:, 136:144].unsqueeze(2), [128, 8, 128]), ALU.mult, [s_o2, s_small], [s_o2])
                tt(d1, d1, d2, ALU.add, [s_o1, s_o2], [s_o1])
                sq3o = sq[:, 0:1024].rearrange("p (a b) -> p a b", a=8)
                tt(sq3o, d1, d1, ALU.mult, [s_o1], [s_sq])
                red(sm[:, 144:152], sq3o, [s_sq], [s_small])
                rsqrt(sm[:, 152:160], sm[:, 144:152], 1.0 / 128, [s_small], [s_small], sm[:, 160:168], s_small)
                ts(sm[:, 152:160], sm[:, 152:160], 1.0 - LAMBDA_INIT, ALU.mult, [s_small], [s_small])
                tt(d1, d1, bc(sm[:, 152:160].unsqueeze(2), [128, 8, 128]), ALU.mult, [s_o1, s_small], [s_o1])
                tt(ocat[:, :, 1024 + h * 128:1024 + (h + 1) * 128], d1, bc(gdo.unsqueeze(1), [128, 8, 128]), ALU.mult,
                   [s_o1, s_const], [s_ocat])

            if n_diff > 1:
                load_wd(1)
            diff_proj(0)
            diff_norm(0)
            diff_tr(0)
            for h in range(n_diff):
                fill = []
                if h + 1 < n_diff:
                    diff_proj(h + 1)
                    if h + 2 < n_diff:
                        load_wd(h + 2)
                    elif stage == "full":
                        for g in range(4):
                            dma("pool", woh[:, :, g * 512:(g + 1) * 512], wo_v[:, 0:NXH, g * 512:(g + 1) * 512],
                                writes=[s_wo[g], s_xT])
                            dma("pool", wol[:, :, g * 512:(g + 1) * 512], wo_v[:, NXH:16, g * 512:(g + 1) * 512],
                                writes=[s_wo[g], s_xT])
                    fill = P.deferred(lambda: diff_norm(h + 1))
                nfill = (len(fill) + 27) // 28
                diff_attn(h, fill, nfill)
                P.release(fill)
                if h + 1 < n_diff:
                    diff_tr(h + 1)
            P.barrier()

        P.skip = False
        if stage != "full":
            A.top = BASE
            stg = [A.alloc([128, D], F32) for _ in range(2)]
            s_stg = [Slot(), Slot()]
            for j in range(8):
                cp(stg[j % 2], ocat[:, j, :], [s_ocat], [s_stg[j % 2]])
                dma("sp", out_d[j * 128:(j + 1) * 128, :], stg[j % 2], reads=[s_stg[j % 2]], writes=[out_slot])
        else:
            A.top = BASE + (16 - NXH) * 4096
            ocatT = A.alloc([128, 16, 1024], BF16)
            s_ocatT = Slot()
            for j in range(8):
                for g in range(2):
                    b = next_bank()
                    tr_group([(psb[b][:, t * 128:(t + 1) * 128], ocat[:, j, (g * 8 + t) * 128:(g * 8 + t + 1) * 128])
                              for t in range(8)], [s_ocat], [bank[b]])
                    cp(ocatT[:, g * 8:(g + 1) * 8, j * 128:(j + 1) * 128],
                       psb[b].rearrange("p (c t) -> p c t", c=8), [bank[b]], [s_ocatT],
                       eng=("act" if (j + g) % 2 else "dve"))
            P.barrier()
            h2T = A.alloc([128, 16, 1024], BF16, at=OCAT_OFF)
            s_h2T = Slot()
            xt_b = [A.alloc([128, D], F32) for _ in range(2)]
            x1_b = [A.alloc([128, D], F32) for _ in range(2)]
            h2_bf = [A.alloc([128, D], BF16) for _ in range(2)]
            gffn = A.alloc([128, D], F32)
            junk = A.alloc([128, D], BF16)
            s_xt = [Slot(), Slot()]
            s_x1 = [Slot(), Slot()]
            s_h2bf = [Slot(), Slot()]
            s_gffn = Slot()
            s_junk = Slot()
            dma("sp", gffn, gffn_d, writes=[s_gffn])
            dma("sp", xt_b[0], xtm_d[0:128, :], writes=[s_xt[0]])
            def p5_mm(j):
                jb = j % 2
                if j + 1 < 8:
                    dma("sp", xt_b[(j + 1) % 2], xtm_d[(j + 1) * 128:(j + 2) * 128, :], writes=[s_xt[(j + 1) % 2]])
                for g in range(4):
                    b = next_bank()
                    mm_group([(psf[b], ocatT[:, c, j * 128:(j + 1) * 128], woc(c)[:, g * 512:(g + 1) * 512], c == 0, c == 15)
                              for c in range(16)], [s_ocatT, s_wo[g]], [bank[b]])
                    tt(x1_b[jb][:, g * 512:(g + 1) * 512], psf[b], xt_b[jb][:, g * 512:(g + 1) * 512], ALU.add,
                       [bank[b], s_xt[jb]], [s_x1[jb]])

            def p5_post(j):
                jb = j % 2
                dma("sp", out_d[j * 128:(j + 1) * 128, :], x1_b[jb], reads=[s_x1[jb]], writes=[out_slot])
                act(junk, x1_b[jb], AF.Square, [s_x1[jb]], [s_junk, s_small], accum_out=sm[:, 168 + j:169 + j])
                rsqrt(sm[:, 176 + j:177 + j], sm[:, 168 + j:169 + j], 1.0 / D, [s_small], [s_small],
                      sm[:, 184 + j:185 + j], s_small)
                P.emit("dve", lambda e, jb=jb, j=j: e.scalar_tensor_tensor(
                    out=h2_bf[jb], in0=x1_b[jb], scalar=sm[:, 176 + j:177 + j], in1=gffn, op0=ALU.mult, op1=ALU.mult),
                    [s_x1[jb], s_small, s_gffn], [s_h2bf[jb]])
                for g in range(2):
                    b = next_bank()
                    tr_group([(psb[b][:, t * 128:(t + 1) * 128], h2_bf[jb][:, (g * 8 + t) * 128:(g * 8 + t + 1) * 128])
                              for t in range(8)], [s_h2bf[jb]], [bank[b]])
                    cp(h2T[:, g * 8:(g + 1) * 8, j * 128:(j + 1) * 128], psb[b].rearrange("p (c t) -> p c t", c=8),
                       [bank[b]], [s_h2T], eng=("act" if g else "dve"))

            p5_mm(0)
            for j in range(8):
                if j + 1 < 8:
                    p5_mm(j + 1)
                p5_post(j)
            P.barrier()
            A.top = BASE
            actT = A.alloc([128, NFC, 1024], BF16)
            s_actT = Slot()
            FG = 256
            NG = DFF // FG
            wgb = [A.alloc([128, 16, FG], BF16) for _ in range(2)]
            wub = [A.alloc([128, 16, FG], BF16) for _ in range(2)]
            sg = [A.alloc([128, 512], F32) for _ in range(2)]
            s_wgb = [Slot(), Slot()]
            s_wub = [Slot(), Slot()]
            s_sg = [Slot(), Slot()]
            G_TOP = A.top
            CG = 256
            wdb0 = A.alloc([128, NFC, CG], BF16)
            s_wdb4 = [[Slot() for _ in range(4)] for _ in range(2)]
            wdn_v = wdn_d.rearrange("(c p) n -> p c n", p=128)

            def load_dn_w(cg, buf):
                for q4 in range(4):
                    dma("pool", buf[:, q4 * 11:(q4 + 1) * 11, :], wdn_v[:, q4 * 11:(q4 + 1) * 11, cg * CG:(cg + 1) * CG],
                        writes=[s_wdb4[cg % 2][q4]])
            wg_v = wg_d.rearrange("(c p) n -> p c n", p=128)
            wu_v = wu_d.rearrange("(c p) n -> p c n", p=128)

            def load_gu(g):
                dma("pool", wgb[g % 2], wg_v[:, :, g * FG:(g + 1) * FG], writes=[s_wgb[g % 2]])
                dma("pool", wub[g % 2], wu_v[:, :, g * FG:(g + 1) * FG], writes=[s_wub[g % 2]])

            load_gu(0)
            k_ = 0
            for g in range(NG):
                gb = g % 2
                if g + 1 < NG:
                    load_gu(g + 1)
                else:
                    load_dn_w(0, wdb0)
                for fl in range(FG // 128):
                    f = g * (FG // 128) + fl
                    for th in range(2):
                        bg = next_bank()
                        bu = next_bank()
                        mm_group([(psf[bg], wgb[gb][:, c, fl * 128:(fl + 1) * 128], h2T[:, c, th * 512:(th + 1) * 512],
                                   c == 0, c == 15) for c in range(16)], [s_wgb[gb], s_h2T], [bank[bg]])
                        mm_group([(psf[bu], wub[gb][:, c, fl * 128:(fl + 1) * 128], h2T[:, c, th * 512:(th + 1) * 512],
                                   c == 0, c == 15) for c in range(16)], [s_wub[gb], s_h2T], [bank[bu]])
                        kb = k_ % 2
                        k_ += 1
                        act(sg[kb], psf[bg], AF.Silu, [bank[bg]], [s_sg[kb]])
                        tt(actT[:, f, th * 512:(th + 1) * 512], sg[kb], psf[bu], ALU.mult, [s_sg[kb], bank[bu]], [s_actT])
            P.barrier()
            A.top = G_TOP - (4 * 16 * FG * 2 + 2 * 2048)
            A.top = BASE + NFC * 1024 * 2
            NCG = D // CG
            wdb = [wdb0, A.alloc([128, NFC, CG], BF16)]
            x1s = [A.alloc([128, 8, CG], F32, at=OCAT_OFF + k * 8 * KB) for k in range(2)]
            ost = [A.alloc([128, 8, CG], F32, at=OCAT_OFF + (2 + k) * 8 * KB) for k in range(2)]
            s_x1s = [Slot(), Slot()]
            s_ost = [Slot(), Slot()]
            out_v = out_d.rearrange("(j p) n -> p j n", p=128)

            def load_dn(cg, w=True):
                if w:
                    load_dn_w(cg, wdb[cg % 2])
                dma("sp", x1s[cg % 2], out_v[:, :, cg * CG:(cg + 1) * CG], reads=[out_slot], writes=[s_x1s[cg % 2]])

            load_dn(0, w=False)
            for cg in range(NCG):
                cb = cg % 2
                if cg + 1 < NCG:
                    load_dn(cg + 1)
                for j in range(8):
                    b = next_bank()
                    mm_group([(psf[b][:, 0:CG], actT[:, f, j * 128:(j + 1) * 128], wdb[cb][:, f, :], f == 0, f == NFC - 1)
                              for f in range(NFC)], [s_actT] + s_wdb4[cb], [bank[b]])
                    tt(ost[cb][:, j, :], psf[b][:, 0:CG], x1s[cb][:, j, :], ALU.add, [bank[b], s_x1s[cb]], [s_ost[cb]])
                dma("sp", out_v[:, :, cg * CG:(cg + 1) * CG], ost[cb], reads=[s_ost[cb], s_x1s[cb]], writes=[out_slot])

        P.barrier(["sp"])
        block = es.enter_context(nc.Block())
        P.replay(block)
    return nc


def _prep_inputs(inp):
    f = np.float32
    x = np.asarray(inp["x"], f)
    w_in = np.asarray(inp["w_in"], f)[0]
    wa = np.ascontiguousarray(w_in[:, :1088])
    dq = w_in[:, 1088:2112].reshape(D, NH, 128)
    dk = w_in[:, 2112:3136].reshape(D, NH, 128)
    dv = w_in[:, 3136:4160].reshape(D, NH, 128)
    wd = np.ascontiguousarray(np.concatenate([dq, dk, dv], axis=2))
    vec = np.zeros(NVEC, f)

    def put(off, name):
        v = np.asarray(inp[name], f)[0]
        vec[off:off + v.shape[0]] = v

    put(V_GQL, "q_latent_norm")
    put(V_GKVL, "kv_latent_norm")
    put(V_GQ, "mla_q_norm")
    put(V_GK, "mla_k_norm")
    put(V_GMO, "mla_out_norm")
    put(V_GDQ, "diff_q_norm")
    put(V_GDK, "diff_k_norm")
    put(V_LQ1, "lambda_q1")
    put(V_LK1, "lambda_k1")
    put(V_LQ2, "lambda_q2")
    put(V_LK2, "lambda_k2")
    put(V_GDO, "diff_out_norm")
    vecs = np.ascontiguousarray(np.broadcast_to(vec[None, :], (128, NVEC)))
    gffn = np.ascontiguousarray(np.broadcast_to(np.asarray(inp["ffn_norm"], f)[0][None, :], (128, D)))
    gcol = np.ascontiguousarray(np.asarray(inp["attn_norm"], f)[0].reshape(16, 128).T)
    wlist = {"wa": wa, "wd": wd.reshape(D, NH * 384), "wqup": np.asarray(inp["w_q_up"], f)[0],
             "wkvup": np.asarray(inp["w_kv_up"], f)[0], "wo": np.asarray(inp["w_o"], f)[0],
             "wg": np.asarray(inp["w_gate"], f)[0], "wu": np.asarray(inp["w_up"], f)[0],
             "wdn": np.asarray(inp["w_down"], f)[0]}
    wts = np.empty(WTS_TOTAL, f)
    for n_, (off, r, c) in WTS_OFF.items():
        assert wlist[n_].shape == (r, c), (n_, wlist[n_].shape)
        wts[off:off + r * c] = wlist[n_].reshape(-1)
    shared = {"wts": wts}
    pos = np.arange(S, dtype=np.float64)
    fm = 1.0 / (500000.0 ** (np.arange(0, 64, 2, dtype=np.float64) / 64))
    fd = 1.0 / (500000.0 ** (np.arange(0, 16, 2, dtype=np.float64) / 16))
    am = pos[:, None] * fm[None, :]
    ad = pos[:, None] * fd[None, :]
    rope_full = np.concatenate([np.cos(am), np.sin(am), np.cos(ad), np.sin(ad)], axis=1).astype(f)
    tri = (np.arange(128)[:, None] <= np.arange(128)[None, :]).astype(f)
    in_maps = []
    perms = []
    for c in range(8):
        b, p = c // 2, c % 2
        own, oth = (G0, G1) if p == 0 else (G1, G0)
        perm = np.concatenate([np.arange(g * 128, (g + 1) * 128) for g in own + oth])
        perms.append((b, perm[:1024]))
        xp = x[b][perm]
        masks = np.zeros((128, 3, 128), f)
        masks[:, 0, :] = tri
        masks[:, 1, :] = 1.0 if p == 1 else 0.0
        masks[:, 2, :] = 1.0 if p == 0 else 0.0
        consts = np.zeros((128, NCONST), f)
        consts[:, C_VECS:C_VECS + NVEC] = vecs
        consts[:, C_GFFN:C_GFFN + D] = gffn
        consts[:, C_GCOL:C_GCOL + 16] = gcol
        consts[:, C_MASK:C_MASK + 384] = masks.reshape(128, 384)
        consts[:, C_IDENT:C_IDENT + 128] = np.eye(128, dtype=f)
        consts[:, C_ROPE:] = rope_full[perm].reshape(16, 128, NROPE).transpose(1, 0, 2).reshape(128, 16 * NROPE)
        m = dict(shared)
        m["xT"] = np.ascontiguousarray(xp.T)
        m["xtm"] = np.ascontiguousarray(xp)
        m["consts"] = consts
        in_maps.append(m)
    return in_maps, perms


_NC_CACHE = {}


def kernel(**inputs):
    in_maps, perms = _prep_inputs(inputs)
    if "full" not in _NC_CACHE:
        _NC_CACHE["full"] = build_nc("full")
    nc = _NC_CACHE["full"]
    res = run_bass_kernel_spmd(nc, in_maps, core_ids=list(range(8)))
    out = np.zeros((4, S, D), np.float32)
    for c in range(8):
        b, rows = perms[c]
        out[b, rows, :] = res.results[c]["out"]
    return out
```
